# Optimizing a Trainium2 kernel written in Bass

```python
import math
import jax, jax.numpy as jnp
from jax import lax
import numpy as np

D_MODEL = 1024
BATCH = 8
SEQ = 4096
DEPTH = 2

CHUNK = 128
GMLP_WIDTH = D_MODEL
GMLP_GROUPS = 8
GMLP_GROUP_DIM = GMLP_WIDTH // GMLP_GROUPS
DA_HEAD_DIM = 64
DA_HEADS = D_MODEL // (2 * DA_HEAD_DIM)
DA_VALUE_WIDTH = DA_HEADS * 2 * DA_HEAD_DIM
Q_BLOCK = 128
REL_BUCKETS = 32
REL_MAX_DISTANCE = 128
D_FF = 2816
CONV_WIDTH = 3
EPS = 1e-6
IN_WIDTH = 2 * GMLP_WIDTH + 2 * (DA_HEADS * 2 * DA_HEAD_DIM) + DA_VALUE_WIDTH

kernel_name = "hybrid_gmlp_diffattn_convglu"


def rms_norm(x, g):
    xf = x.astype(jnp.float32)
    y = xf * lax.rsqrt(jnp.mean(xf * xf, axis=-1, keepdims=True) + EPS)
    return (y * g.astype(jnp.float32)).astype(x.dtype)


def t5_causal_bucket(rel):
    n = jnp.maximum(rel, 0)
    max_exact = REL_BUCKETS // 2
    nf = jnp.maximum(n, 1).astype(jnp.float32)
    large = max_exact + (jnp.log(nf / max_exact) / math.log(REL_MAX_DISTANCE / max_exact)
                         * (REL_BUCKETS - max_exact)).astype(jnp.int32)
    large = jnp.minimum(large, REL_BUCKETS - 1)
    return jnp.where(n < max_exact, n, large)


def gmlp_spatial_gate(u, v, v_norm_g, w_s, b_s):
    B, S, _ = v.shape
    nc = S // CHUNK
    v = rms_norm(v, v_norm_g)
    vc = v.reshape(B, nc, CHUNK, GMLP_GROUPS, GMLP_GROUP_DIM)
    causal = jnp.tril(jnp.ones((CHUNK, CHUNK), dtype=w_s.dtype))
    w_m = w_s * causal[None]
    mixed = jnp.einsum('gij,bnjgc->bnigc', w_m, vc) + b_s.T[None, None, :, :, None]
    return u * mixed.reshape(B, S, GMLP_WIDTH)


def diff_attention(q, k, v, lam, rel_bias):
    B, S = q.shape[0], q.shape[1]
    nb = S // Q_BLOCK
    scale = DA_HEAD_DIM ** -0.5
    k1 = k[:, :, :, 0].transpose(0, 2, 1, 3)
    k2 = k[:, :, :, 1].transpose(0, 2, 1, 3)
    vh = v.transpose(0, 2, 1, 3)
    qb = q.reshape(B, nb, Q_BLOCK, DA_HEADS, 2, DA_HEAD_DIM).transpose(1, 0, 3, 2, 4, 5)
    k_pos = jnp.arange(S)
    lam32 = lam.astype(jnp.float32)

    def block(args):
        i, qblk = args
        q_pos = i * Q_BLOCK + jnp.arange(Q_BLOCK)
        rel = q_pos[:, None] - k_pos[None, :]
        mask = rel >= 0
        bias = jnp.take(rel_bias, t5_causal_bucket(rel), axis=0)
        bias = bias.transpose(2, 0, 1).astype(jnp.float32)[None]
        s1 = jnp.einsum('bhqd,bhkd->bhqk', qblk[:, :, :, 0], k1).astype(jnp.float32) * scale + bias
        s2 = jnp.einsum('bhqd,bhkd->bhqk', qblk[:, :, :, 1], k2).astype(jnp.float32) * scale + bias
        a1 = jax.nn.softmax(jnp.where(mask, s1, -jnp.inf), axis=-1)
        a2 = jax.nn.softmax(jnp.where(mask, s2, -jnp.inf), axis=-1)
        a = (a1 - lam32 * a2).astype(vh.dtype)
        return jnp.einsum('bhqk,bhkc->bhqc', a, vh)

    out = lax.map(block, (jnp.arange(nb), qb))
    return out.transpose(1, 0, 3, 2, 4).reshape(B, S, DA_HEADS, 2 * DA_HEAD_DIM)


def causal_dwconv(a, w, b):
    S = a.shape[1]
    ap = jnp.pad(a, ((0, 0), (CONV_WIDTH - 1, 0), (0, 0)))
    out = b
    for j in range(CONV_WIDTH):
        out = out + w[j] * ap[:, j:j + S]
    return out


def setup_inputs(seed: int = 0) -> dict:
    key = jax.random.key(seed)
    ks = jax.random.split(key, 24)
    L = DEPTH

    def nrm(k, shape, scale):
        return jax.random.normal(k, shape, jnp.float32) * scale

    def gain(k, shape):
        return 1.0 + 0.02 * jax.random.normal(k, shape, jnp.float32)

    return {
        "x": jax.random.normal(ks[0], (BATCH, SEQ, D_MODEL), jnp.float32),
        "norm1_g": gain(ks[1], (L, D_MODEL)),
        "w_in": nrm(ks[2], (L, D_MODEL, IN_WIDTH), D_MODEL ** -0.5),
        "w_gate": nrm(ks[3], (L, D_MODEL, 2 * D_MODEL), D_MODEL ** -0.5),
        "gmlp_vnorm_g": gain(ks[4], (L, GMLP_WIDTH)),
        "gmlp_ws": nrm(ks[5], (L, GMLP_GROUPS, CHUNK, CHUNK), CHUNK ** -0.5),
        "gmlp_b": 1.0 + 0.02 * jax.random.normal(ks[6], (L, GMLP_GROUPS, CHUNK), jnp.float32),
        "lam_q1": nrm(ks[7], (L, DA_HEAD_DIM), 0.1),
        "lam_k1": nrm(ks[8], (L, DA_HEAD_DIM), 0.1),
        "lam_q2": nrm(ks[9], (L, DA_HEAD_DIM), 0.1),
        "lam_k2": nrm(ks[10], (L, DA_HEAD_DIM), 0.1),
        "subln_g": gain(ks[11], (L, 2 * DA_HEAD_DIM)),
        "rel_bias": nrm(ks[12], (REL_BUCKETS, DA_HEADS), 0.5),
        "w_a": nrm(ks[13], (L, GMLP_WIDTH, D_MODEL), GMLP_WIDTH ** -0.5),
        "w_b": nrm(ks[14], (L, DA_VALUE_WIDTH, D_MODEL), DA_VALUE_WIDTH ** -0.5),
        "w_out": nrm(ks[15], (L, D_MODEL, D_MODEL), D_MODEL ** -0.5),
        "norm2_g": gain(ks[16], (L, D_MODEL)),
        "w_up": nrm(ks[17], (L, D_MODEL, 2 * D_FF), D_MODEL ** -0.5),
        "conv_w": nrm(ks[18], (L, CONV_WIDTH, D_FF), CONV_WIDTH ** -0.5),
        "conv_b": nrm(ks[19], (L, D_FF), 0.02),
        "w_down": nrm(ks[20], (L, D_FF, D_MODEL), D_FF ** -0.5),
        "final_g": gain(ks[21], (D_MODEL,)),
    }


def reference(x, norm1_g, w_in, w_gate, gmlp_vnorm_g, gmlp_ws, gmlp_b, lam_q1, lam_k1,
              lam_q2, lam_k2, subln_g, rel_bias, w_a, w_b, w_out, norm2_g, w_up,
              conv_w, conv_b, w_down, final_g):
    B, S, _ = x.shape
    qk_w = DA_HEADS * 2 * DA_HEAD_DIM
    splits = np.cumsum([GMLP_WIDTH, GMLP_WIDTH, qk_w, qk_w]).tolist()
    for l in range(DEPTH):
        h = rms_norm(x, norm1_g[l])
        proj = jnp.einsum('bsd,de->bse', h, w_in[l])
        uv_u, uv_v, q, k, v = jnp.split(proj, splits, axis=-1)
        gates = jax.nn.sigmoid(jnp.einsum('bsd,de->bse', h, w_gate[l]))
        g_a, g_b = jnp.split(gates, 2, axis=-1)

        y_a = gmlp_spatial_gate(jax.nn.gelu(uv_u), jax.nn.gelu(uv_v),
                                gmlp_vnorm_g[l], gmlp_ws[l], gmlp_b[l])

        lam_init = 0.8 - 0.6 * math.exp(-0.3 * l)
        lam = (jnp.exp(jnp.sum(lam_q1[l].astype(jnp.float32) * lam_k1[l].astype(jnp.float32)))
               - jnp.exp(jnp.sum(lam_q2[l].astype(jnp.float32) * lam_k2[l].astype(jnp.float32)))
               + lam_init)
        q = q.reshape(B, S, DA_HEADS, 2, DA_HEAD_DIM)
        k = k.reshape(B, S, DA_HEADS, 2, DA_HEAD_DIM)
        v = v.reshape(B, S, DA_HEADS, 2 * DA_HEAD_DIM)
        o = diff_attention(q, k, v, lam, rel_bias)
        o = rms_norm(o, subln_g[l]) * (1.0 - lam_init)
        y_b = o.reshape(B, S, DA_VALUE_WIDTH)

        merged = (g_a * jnp.einsum('bsc,cd->bsd', y_a, w_a[l])
                  + g_b * jnp.einsum('bsc,cd->bsd', y_b, w_b[l]))
        x = x + jnp.einsum('bsd,de->bse', merged, w_out[l])

        h2 = rms_norm(x, norm2_g[l])
        up = jnp.einsum('bsd,df->bsf', h2, w_up[l])
        a, bval = jnp.split(up, 2, axis=-1)
        a = causal_dwconv(a, conv_w[l], conv_b[l])
        x = x + jnp.einsum('bsf,fd->bsd', jax.nn.gelu(a) * bval, w_down[l])
    return rms_norm(x, final_g)
```

```python
import math
from contextlib import ExitStack

import numpy as np
import concourse.bass as bass
import concourse.mybir as mybir
from concourse.bass_utils import run_bass_kernel_spmd

F32 = mybir.dt.float32
BF16 = mybir.dt.bfloat16
AF = mybir.ActivationFunctionType
ALU = mybir.AluOpType

NCORES = 8
S = 4096
D = 1024
L = 2
DFF = 2816
NF = DFF // 128
H = 8
EPS = 1e-6
TT = 512
NT = S // TT
TE = 256
ENG = ("pe", "act", "dve", "pool", "sp")
NDS_SP = 28
NDS_POOL = 64


class Ev:
    __slots__ = ("sem", "val", "ds")

    def __init__(self, sem, val, ds=None):
        self.sem = sem
        self.val = val
        self.ds = ds


class Res:
    __slots__ = ("name", "w", "r")

    def __init__(self, name=""):
        self.name = name
        self.w = None
        self.r = {}


class DSem:
    def __init__(self, sem, q):
        self.sem = sem
        self.q = q
        self.cnt = 0
        self.drained = 0


class Sched:
    def __init__(self, nc):
        self.nc = nc
        self.sem = {e: nc.alloc_semaphore(name="sem_" + e) for e in ENG}
        self.cnt = {e: 0 for e in ENG}
        self.waited = {e: {} for e in ENG}
        self.stream = {e: [] for e in ENG}
        self.pe_pending = []
        self.dsq = {"sp": [DSem(nc.alloc_semaphore(name="dsp%d" % i), "sp") for i in range(NDS_SP)],
                    "pool": [DSem(nc.alloc_semaphore(name="dpl%d" % i), "pool") for i in range(NDS_POOL)]}
        self.dsems = self.dsq["sp"] + self.dsq["pool"]
        self.ds_next = {"sp": 0, "pool": 0}

    def new_ds(self, q="sp"):
        lst = self.dsq[q]
        ds = lst[self.ds_next[q] % len(lst)]
        self.ds_next[q] += 1
        return ds

    def _wait(self, e, ev):
        if ev is None:
            return
        if e == "pe" and ev.sem is self.sem["pe"]:
            return
        val = ev.ds.cnt if ev.ds is not None else ev.val
        assert val is not None, "waiting on an unsignaled PE instruction"
        w = self.waited[e]
        key = ev.sem.num
        if w.get(key, 0) >= val:
            return
        w[key] = val
        self.stream[e].append(("w", ev.sem, val))

    def _deps(self, e, R, W):
        for r in R:
            self._wait(e, r.w)
        for w in W:
            self._wait(e, w.w)
            for ev in w.r.values():
                self._wait(e, ev)

    def _commit(self, ev, R, W):
        for r in R:
            r.r[ev.sem.num] = ev
        for w in W:
            w.w = ev
            w.r = {}

    def op(self, e, fn, R=(), W=(), sig=True):
        self._deps(e, R, W)
        ev = Ev(self.sem[e], None)
        if sig:
            self.cnt[e] += 1
            ev.val = self.cnt[e]
            if e == "pe":
                for p in self.pe_pending:
                    p.val = ev.val
                self.pe_pending = []
            self.stream[e].append(("o", fn, self.sem[e]))
        else:
            assert e == "pe"
            self.pe_pending.append(ev)
            self.stream[e].append(("o", fn, None))
        self._commit(ev, R, W)

    def dma(self, q, out, in_, R=(), W=(), ds=None):
        assert ds.q == q, (ds.q, q)
        self._deps(q, R, W)
        ds.cnt += 16
        ev = Ev(ds.sem, ds.cnt, ds)
        self.stream[q].append(("d", out, in_, ds.sem))
        self._commit(ev, R, W)

    def mm(self, out, lhsT, rhs, start, stop, R=(), W=(), sig=None, skip=False):
        if sig is None:
            sig = stop
        if skip:
            fn = lambda h, o=out, l=lhsT, r=rhs, s=start, t=stop: h.matmul(o, l, r, start=s, stop=t, skip_group_check=True)
        else:
            fn = lambda h, o=out, l=lhsT, r=rhs, s=start, t=stop: h.matmul(o, l, r, start=s, stop=t)
        self.op("pe", fn, R, W, sig)

    def tr(self, out, in_, ident, R=(), W=(), sig=True):
        self.op("pe", lambda h, o=out, i=in_, d=ident: h.transpose(o, i, d), R, W, sig)

    def act(self, out, in_, func, R=(), W=(), **kw):
        self.op("act", lambda h, o=out, i=in_, f=func, k=kw: h.activation(out=o, in_=i, func=f, **k), R, W)

    def tt(self, e, out, in0, in1, op, R=(), W=()):
        self.op(e, lambda h, o=out, a=in0, b=in1, p=op: h.tensor_tensor(out=o, in0=a, in1=b, op=p), R, W)

    def ts(self, e, out, in0, s1, s2, op0, op1=None, R=(), W=()):
        if op1 is None:
            fn = lambda h, o=out, a=in0, x=s1, p=op0: h.tensor_scalar(out=o, in0=a, scalar1=x, scalar2=None, op0=p)
        else:
            fn = lambda h, o=out, a=in0, x=s1, y=s2, p=op0, q=op1: h.tensor_scalar(out=o, in0=a, scalar1=x, scalar2=y, op0=p, op1=q)
        self.op(e, fn, R, W)

    def stt(self, e, out, in0, scalar, in1, op0, op1, R=(), W=()):
        self.op(e, lambda h, o=out, a=in0, s=scalar, b=in1, p=op0, q=op1: h.scalar_tensor_tensor(out=o, in0=a, scalar=s, in1=b, op0=p, op1=q), R, W)

    def copy(self, e, out, in_, R=(), W=()):
        if e == "act":
            self.op(e, lambda h, o=out, i=in_: h.activation(out=o, in_=i, func=AF.Copy), R, W)
        else:
            self.op(e, lambda h, o=out, i=in_: h.tensor_copy(out=o, in_=i), R, W)

    def memset(self, e, ap, val, R=(), W=()):
        self.op(e, lambda h, a=ap, v=val: h.memset(a, v), R, W)

    def recip(self, out, in_, R=(), W=()):
        self.op("dve", lambda h, o=out, i=in_: h.reciprocal(out=o, in_=i), R, W)

    def flush(self):
        for ds in self.dsems:
            if ds.cnt > ds.drained:
                self.stream["sp"].append(("w", ds.sem, ds.cnt))
                ds.drained = ds.cnt
        streams = self.stream
        self.stream = {e: [] for e in ENG}

        def run(h, items):
            for it in items:
                if it[0] == "w":
                    h.wait_ge(it[1], it[2])
                elif it[0] == "o":
                    ins = it[1](h)
                    if it[2] is not None:
                        ins.then_inc(it[2], 1)
                else:
                    h.dma_start(out=it[1], in_=it[2]).then_inc(it[3], 16)

        with self.nc.Block() as block:
            @block.tensor
            def _(h):
                run(h, streams["pe"])

            @block.scalar
            def _(h):
                run(h, streams["act"])

            @block.vector
            def _(h):
                run(h, streams["dve"])

            @block.gpsimd
            def _(h):
                run(h, streams["pool"])

            @block.sync
            def _(h):
                run(h, streams["sp"])


class Phase:
    def __init__(self, nc, tag):
        self.nc = nc
        self.tag = tag
        self.es = ExitStack()
        self.n = 0

    def sb(self, shape, dt, name=None):
        self.n += 1
        return self.es.enter_context(self.nc.sbuf_tensor("%s_%s%d" % (self.tag, name or "t", self.n), shape, dt))

    def ps(self, shape, dt, name=None):
        self.n += 1
        return self.es.enter_context(self.nc.psum_tensor("%s_%s%d" % (self.tag, name or "p", self.n), shape, dt))

    def close(self):
        self.es.close()


def chunked(ap2d):
    return ap2d.rearrange("(kc p) t -> p kc t", p=128)


def build_program(dbg=False, phases=None, nlayers=L):
    nc = bass.Bass("TRN2", target_bir_lowering=False)
    sc = Sched(nc)

    def din(name, shape, dt=F32):
        return nc.dram_tensor(name, shape, dt, kind="ExternalInput").ap()

    def dscr(name, shape, dt):
        if dbg:
            return nc.dram_tensor(name, shape, dt, kind="ExternalOutput").ap()
        return nc.dram_tensor(name, shape, dt).ap()

    x_in = din("x", [S, D])
    w_in = din("w_in", [L, D, 5120])
    w_gate = din("w_gate", [L, D, 2048])
    w_a = din("w_a", [L, D, D])
    w_b = din("w_b", [L, D, D])
    w_out = din("w_out", [L, D, D])
    w_up = din("w_up", [L, D, 2 * DFF])
    w_down = din("w_down", [L, DFF, D])
    g1T = din("g1T", [L, 128, 8])
    g2T = din("g2T", [L, 128, 8])
    vng = din("vng", [L, D])
    wsT = din("wsT", [L, 8, 128, 128])
    gb = din("gmlp_b", [L, 1, D])
    lamv = din("lamv", [L, 4, 64])
    subg = din("subln_g", [L, 128])
    relb = din("rel_bias", [32, H])
    btab = din("btab", [128, H, 256])
    convT = din("convT", [L, 128, NF, 4])
    fing = din("final_g", [1, D])
    c_ident = din("c_ident", [128, 128])
    c_tri = din("c_tri", [128, 128])
    c_mask = din("c_mask", [128, 256])
    out = nc.dram_tensor("out", [S, D], F32, kind="ExternalOutput").ap()

    xT = dscr("xT", [D, S], F32)
    hT = dscr("hT", [D, S], BF16)
    m1T = dscr("m1T", [D, S], F32)
    QT = dscr("QT", [D, S], BF16)
    KT = dscr("KT", [D, S], BF16)
    Vt = dscr("Vt", [S, D], BF16)
    ybT = dscr("ybT", [D, S], BF16)

    def want(p):
        return phases is None or p in phases

    class WT:
        def __init__(self, dst, src2d, bounds):
            self.t = dst
            self.v = src2d.rearrange("(kc p) e -> p kc e", p=128)
            self.b = bounds
            self.r = [Res() for _ in range(len(bounds) - 1)]
            self.issued = 0

        def issue(self, n=None):
            n = len(self.r) - self.issued if n is None else n
            for _ in range(n):
                i = self.issued
                c0, c1 = self.b[i], self.b[i + 1]
                sc.dma("pool", self.t[:, :, c0:c1], self.v[:, :, c0:c1], W=[self.r[i]], ds=sc.new_ds("pool"))
                self.issued += 1

        def res(self, c):
            for i in range(len(self.r)):
                if self.b[i] <= c < self.b[i + 1]:
                    return self.r[i]
            raise KeyError(c)

    def load_w(ph, dst, src2d, res, ds, kc_n, q="pool"):
        v = src2d.rearrange("(kc p) e -> p kc e", p=128)
        for kc in range(kc_n):
            sc.dma(q, dst[:, kc, :], v[:, kc, :], W=[res], ds=ds)

    def rms_stats(ph, xt, xt_res, sq, sq_res, bank, bank_res, rstd, rstd_res, ones_bf, ones_res, ncol, nk=8):
        for kc in range(nk):
            sc.act(sq[:, kc, :], xt[:, kc, :], AF.Square, R=[xt_res], W=[sq_res])
        for kc in range(nk):
            sc.mm(bank, ones_bf[:], sq[:, kc, :], kc == 0, kc == nk - 1, R=[sq_res, ones_res], W=[bank_res])
        sc.ts("dve", rstd, bank, 1.0 / D, EPS, ALU.mult, ALU.add, R=[bank_res], W=[rstd_res])
        sc.act(rstd, rstd, AF.Sqrt, R=[rstd_res], W=[rstd_res])
        sc.recip(rstd, rstd, R=[rstd_res], W=[rstd_res])

    if want("T0"):
        ph = Phase(nc, "t0")
        ident = ph.sb([128, 128], F32, "ident")
        r_ident = Res()
        ds_c = sc.new_ds()
        sc.dma("sp", ident[:], c_ident, W=[r_ident], ds=ds_c)
        NB3 = 3
        xin = [ph.sb([128, 4, D], F32, "xin") for _ in range(NB3)]
        r_xin = [Res() for _ in range(NB3)]
        ds_xin = [sc.new_ds() for _ in range(NB3)]
        xo = [ph.sb([128, 8, TT], F32, "xo") for _ in range(NB3)]
        r_xo = [Res() for _ in range(NB3)]
        ds_xo = [sc.new_ds("pool") for _ in range(NB3)]
        banks = [ph.ps([128, 512], F32, "bk") for _ in range(8)]
        r_banks = [Res() for _ in range(8)]
        xv = x_in.rearrange("(n s p) d -> n p s d", p=128, s=4)
        xTv = chunked(xT)
        bi = 0
        sc.dma("sp", xin[0][:], xv[0], W=[r_xin[0]], ds=ds_xin[0])
        sc.dma("sp", xin[1][:], xv[1], W=[r_xin[1]], ds=ds_xin[1])
        for t in range(NT):
            b = t % NB3
            if t + 2 < NT:
                n2 = (t + 2) % NB3
                sc.dma("sp", xin[n2][:], xv[t + 2], W=[r_xin[n2]], ds=ds_xin[n2])
            for kc in range(8):
                bk = bi % 8
                bi += 1
                for s_ in range(4):
                    sc.tr(banks[bk][:, s_ * 128:(s_ + 1) * 128], xin[b][:, s_, kc * 128:(kc + 1) * 128], ident[:],
                          R=[r_xin[b], r_ident], W=[r_banks[bk]], sig=(s_ == 3))
                sc.copy("act" if kc % 2 == 0 else "dve", xo[b][:, kc, :], banks[bk][:], R=[r_banks[bk]], W=[r_xo[b]])
            sc.dma("pool", xTv[:, :, t * TT:(t + 1) * TT], xo[b][:], R=[r_xo[b]], ds=ds_xo[b])
        sc.flush()
        ph.close()

    for l in range(nlayers):
        lam_init = 0.8 - 0.6 * math.exp(-0.3 * l)
        if want("A"):
            ph = Phase(nc, "a%d" % l)
            Wuv = ph.sb([128, 8, 2048], BF16, "wuv")
            Wga = ph.sb([128, 8, 1024], BF16, "wga")
            Wa = ph.sb([128, 8, 1024], BF16, "wa")
            Wm32 = ph.sb([128, 8, 128], F32, "wm32")
            tri = ph.sb([128, 128], F32, "tri")
            Wm = ph.sb([128, 8, 128], BF16, "wm")
            bias_bc = ph.sb([128, 8, 128], F32, "biasbc")
            mtmp = [ph.sb([128, 4, 128], F32, "mtmp") for _ in range(2)]
            r_mtmp = [Res() for _ in range(2)]
            ones_bf = ph.sb([128, 128], BF16, "ones")
            vng_bc = ph.sb([128, D], F32, "vng")
            g1 = ph.sb([128, 8], F32, "g1")
            r_wm, r_c, r_b, r_ones = Res(), Res(), Res(), Res()
            ds_c = sc.new_ds()
            xt = [ph.sb([128, 8, TT], F32, "xt") for _ in range(2)]
            r_xt = [Res() for _ in range(2)]
            ds_xt = [sc.new_ds() for _ in range(2)]
            xTv = chunked(xT)
            hTv = chunked(hT)
            sc.dma("sp", xt[0][:], xTv[:, :, 0:TT], W=[r_xt[0]], ds=ds_xt[0])
            sc.dma("sp", g1[:], g1T[l], W=[r_c], ds=ds_c)
            sc.dma("sp", vng_bc[:], vng[l:l + 1, :].partition_broadcast(128), W=[r_c], ds=ds_c)
            sc.dma("sp", tri[:], c_tri, W=[r_c], ds=ds_c)
            sc.dma("sp", Wm32[:], wsT[l].rearrange("g j i -> j g i"), W=[r_c], ds=ds_c)
            sc.dma("sp", bias_bc[:], gb[l].partition_broadcast(128).rearrange("p o (g i) -> p (o g) i", g=8), W=[r_c], ds=ds_c)
            sc.memset("pool", ones_bf[:], 1.0, W=[r_ones])
            wuv = WT(Wuv, w_in[l][:, 0:2048], [0, 256, 512, 768, 1024, 1536, 2048])
            wga = WT(Wga, w_gate[l][:, 0:1024], [0, 256, 512, 768, 1024])
            wa = WT(Wa, w_a[l], [0, 256, 512, 768, 1024])
            wuv.issue()
            wga.issue()
            wa.issue()
            for g in range(8):
                sc.tt("dve", Wm[:, g, :], Wm32[:, g, :], tri[:], ALU.mult, R=[r_c], W=[r_wm])

            sq = ph.sb([128, 8, TT], BF16, "sq")
            r_sq = Res()
            rstd = ph.sb([128, TT], F32, "rstd")
            r_rstd = Res()
            hTt = [ph.sb([128, 8, TT], BF16, "hT") for _ in range(2)]
            r_hT = [Res() for _ in range(2)]
            ds_hT = [sc.new_ds("pool") for _ in range(2)]
            guT = ph.sb([128, 8, TT], F32, "guT")
            r_gu = [Res() for _ in range(8)]
            gv = [ph.sb([128, D], F32, "gv") for _ in range(2)]
            r_gv = [Res() for _ in range(2)]
            junk = ph.sb([128, D], BF16, "junk")
            r_junk = Res()
            ssv = ph.sb([128, 8], F32, "ssv")
            r_ssv = [Res() for _ in range(2)]
            vn = ph.sb([128, 4, D], BF16, "vn")
            r_vn = [Res() for _ in range(4)]
            yaT = ph.sb([128, 8, TT], BF16, "yaT")
            r_ya = [Res() for _ in range(8)]
            gaT = ph.sb([128, 8, TT], F32, "gaT")
            r_ga = [Res() for _ in range(8)]
            m1 = [ph.sb([128, TT], F32, "m1") for _ in range(2)]
            r_m1 = [Res() for _ in range(2)]
            ds_m1 = [sc.new_ds("pool") for _ in range(2)]
            banks = [ph.ps([128, 512], F32, "bk") for _ in range(8)]
            r_bk = [Res() for _ in range(8)]
            bi = [0]

            def nb():
                k = bi[0] % 8
                bi[0] += 1
                return k

            def pro_sq(t):
                b = t % 2
                for kc in range(8):
                    sc.act(sq[:, kc, :], xt[b][:, kc, :], AF.Square, R=[r_xt[b]], W=[r_sq])

            def pro_stats(t):
                k = nb()
                for kc in range(8):
                    sc.mm(banks[k][:], ones_bf[:], sq[:, kc, :], kc == 0, kc == 7, R=[r_sq, r_ones], W=[r_bk[k]])
                sc.ts("dve", rstd[:], banks[k][:], 1.0 / D, EPS, ALU.mult, ALU.add, R=[r_bk[k]], W=[r_rstd])
                sc.act(rstd[:], rstd[:], AF.Sqrt, R=[r_rstd], W=[r_rstd])
                sc.recip(rstd[:], rstd[:], R=[r_rstd], W=[r_rstd])

            def pro_h1(t, kc):
                b = t % 2
                sc.stt("dve", hTt[b][:, kc, :], xt[b][:, kc, :], g1[:, kc:kc + 1], rstd[:], ALU.mult, ALU.mult,
                       R=[r_xt[b], r_rstd, r_c], W=[r_hT[b]])
                if kc == 7:
                    sc.dma("pool", hTv[:, :, t * TT:(t + 1) * TT], hTt[b][:], R=[r_hT[b]], ds=ds_hT[b])

            def pro_h(t):
                pro_stats(t)
                for kc in range(8):
                    pro_h1(t, kc)

            pro_sq(0)
            pro_h(0)
            ei = 0
            for t in range(NT):
                b = t % 2
                tok = slice(t * TT, (t + 1) * TT)
                hc = hTt[b]
                if t + 1 < NT:
                    sc.dma("sp", xt[1 - b][:], xTv[:, :, (t + 1) * TT:(t + 2) * TT], W=[r_xt[1 - b]], ds=ds_xt[1 - b])
                for e in range(8):
                    k = nb()
                    for kc in range(8):
                        sc.mm(banks[k][:], Wuv[:, kc, e * 128:(e + 1) * 128], hc[:, kc, :], kc == 0, kc == 7,
                              R=[wuv.res(e * 128), r_hT[b]], W=[r_bk[k]])
                    sc.act(guT[:, e, :], banks[k][:], AF.Gelu_apprx_tanh, R=[r_bk[k]], W=[r_gu[e]])
                for s in range(4):
                    gs = s % 2
                    for half in range(2):
                        k = nb()
                        for kc in range(8):
                            sc.mm(banks[k][:], hc[:, kc, s * 128:(s + 1) * 128],
                                  Wuv[:, kc, 1024 + half * 512:1024 + (half + 1) * 512], kc == 0, kc == 7,
                                  R=[wuv.res(1024 + half * 512), r_hT[b]], W=[r_bk[k]])
                        sc.act(gv[gs][:, half * 512:(half + 1) * 512], banks[k][:], AF.Gelu_apprx_tanh,
                               R=[r_bk[k]], W=[r_gv[gs]])
                        if t + 1 < NT:
                            kq = s * 2 + half
                            sc.act(sq[:, kq, :], xt[1 - b][:, kq, :], AF.Square, R=[r_xt[1 - b]], W=[r_sq])
                    sc.act(junk[:], gv[gs][:], AF.Square, R=[r_gv[gs]], W=[r_junk, r_ssv[gs]],
                           accum_out=ssv[:, gs * 4:gs * 4 + 1])
                    sc.ts("dve", ssv[:, gs * 4 + 1:gs * 4 + 2], ssv[:, gs * 4:gs * 4 + 1], 1.0 / D, EPS, ALU.mult, ALU.add,
                          R=[r_ssv[gs]], W=[r_ssv[gs]])
                    sc.act(ssv[:, gs * 4 + 2:gs * 4 + 3], ssv[:, gs * 4 + 1:gs * 4 + 2], AF.Sqrt, R=[r_ssv[gs]], W=[r_ssv[gs]])
                    sc.recip(ssv[:, gs * 4 + 3:gs * 4 + 4], ssv[:, gs * 4 + 2:gs * 4 + 3], R=[r_ssv[gs]], W=[r_ssv[gs]])
                    sc.stt("dve", vn[:, s, :], gv[gs][:], ssv[:, gs * 4 + 3:gs * 4 + 4], vng_bc[:], ALU.mult, ALU.mult,
                           R=[r_gv[gs], r_ssv[gs], r_c], W=[r_vn[s]])
                if t + 1 < NT:
                    pro_stats(t + 1)
                def gates(e0, e1):
                    for e in range(e0, e1):
                        kB = nb()
                        for kc in range(8):
                            sc.mm(banks[kB][:], Wga[:, kc, e * 128:(e + 1) * 128], hc[:, kc, :], kc == 0, kc == 7,
                                  R=[wga.res(e * 128), r_hT[b]], W=[r_bk[kB]])
                        sc.act(gaT[:, e, :], banks[kB][:], AF.Sigmoid, R=[r_bk[kB]], W=[r_ga[e]])
                        if t + 1 < NT:
                            pro_h1(t + 1, e)

                gates(0, 4)
                for g in range(8):
                    k = nb()
                    for s in range(4):
                        o = banks[k][:, s * 128:(s + 1) * 128]
                        sc.mm(o, vn[:, s, g * 128:(g + 1) * 128], Wm[:, g, :], True, True, R=[r_vn[s], r_wm], W=[r_bk[k]], sig=(s == 3))
                    mb = g % 2
                    sc.tt("dve", mtmp[mb][:], banks[k][:].rearrange("p (s i) -> p s i", i=128),
                          bias_bc[:, g, :].unsqueeze(1).to_broadcast([128, 4, 128]), ALU.add,
                          R=[r_bk[k], r_c], W=[r_mtmp[mb]])
                    sc.tt("dve", yaT[:, g, :], mtmp[mb][:].rearrange("p s i -> p (s i)"), guT[:, g, :], ALU.mult,
                          R=[r_mtmp[mb], r_gu[g]], W=[r_ya[g]])
                gates(4, 8)
                for e in range(8):
                    eb = ei % 2
                    ei += 1
                    kA = nb()
                    for g in range(8):
                        sc.mm(banks[kA][:], Wa[:, g, e * 128:(e + 1) * 128], yaT[:, g, :], g == 0, g == 7,
                              R=[wa.res(e * 128), r_ya[g]], W=[r_bk[kA]])
                    sc.tt("dve", m1[eb][:], banks[kA][:], gaT[:, e, :], ALU.mult, R=[r_bk[kA], r_ga[e]], W=[r_m1[eb]])
                    sc.dma("pool", m1T[e * 128:(e + 1) * 128, tok], m1[eb][:], R=[r_m1[eb]], ds=ds_m1[eb])
            sc.flush()
            ph.close()


        if want("B"):
            ph = Phase(nc, "b%d" % l)
            Wq = ph.sb([128, 8, 3072], BF16, "wqkv")
            wq = WT(Wq, w_in[l][:, 2048:5120], list(range(0, 2048, 256)) + [2048, 2560, 3072])
            wq.issue()
            hb = [ph.sb([128, 8, TT], BF16, "hb") for _ in range(2)]
            r_hb = [Res() for _ in range(2)]
            ds_hb = [sc.new_ds() for _ in range(2)]
            qo = [ph.sb([128, 8, TT], BF16, "qo") for _ in range(2)]
            ko = [ph.sb([128, 8, TT], BF16, "ko") for _ in range(2)]
            vo = [ph.sb([128, 4, D], BF16, "vo") for _ in range(2)]
            r_qo = [Res() for _ in range(2)]
            r_ko = [Res() for _ in range(2)]
            r_vo = [Res() for _ in range(2)]
            ds_o = [sc.new_ds("pool") for _ in range(2)]
            banks = [ph.ps([128, 512], F32, "bk") for _ in range(8)]
            r_bk = [Res() for _ in range(8)]
            bi = 0
            hTv = chunked(hT)
            QTv = chunked(QT)
            KTv = chunked(KT)
            Vv = Vt.rearrange("(n p) c -> p n c", p=128)
            sc.dma("sp", hb[0][:], hTv[:, :, 0:TT], W=[r_hb[0]], ds=ds_hb[0])
            for t in range(NT):
                b = t % 2
                tok = slice(t * TT, (t + 1) * TT)
                if t + 1 < NT:
                    sc.dma("sp", hb[1 - b][:], hTv[:, :, (t + 1) * TT:(t + 2) * TT], W=[r_hb[1 - b]], ds=ds_hb[1 - b])
                for e in range(16):
                    k = bi % 8
                    bi += 1
                    for kc in range(8):
                        sc.mm(banks[k][:], Wq[:, kc, e * 128:(e + 1) * 128], hb[b][:, kc, :], kc == 0, kc == 7,
                              R=[wq.res(e * 128), r_hb[b]], W=[r_bk[k]])
                    if e < 8:
                        sc.act(qo[b][:, e, :], banks[k][:], AF.Copy, R=[r_bk[k]], W=[r_qo[b]], scale=0.125)
                    else:
                        sc.copy("dve", ko[b][:, e - 8, :], banks[k][:], R=[r_bk[k]], W=[r_ko[b]])
                sc.dma("pool", QTv[:, :, tok], qo[b][:], R=[r_qo[b]], ds=ds_o[b])
                sc.dma("pool", KTv[:, :, tok], ko[b][:], R=[r_ko[b]], ds=ds_o[b])
                for s in range(4):
                    for half in range(2):
                        k = bi % 8
                        bi += 1
                        for kc in range(8):
                            sc.mm(banks[k][:], hb[b][:, kc, s * 128:(s + 1) * 128],
                                  Wq[:, kc, 2048 + half * 512:2048 + (half + 1) * 512], kc == 0, kc == 7,
                                  R=[wq.res(2048 + half * 512), r_hb[b]], W=[r_bk[k]])
                        sc.copy("act" if half == 0 else "dve", vo[b][:, s, half * 512:(half + 1) * 512], banks[k][:],
                                R=[r_bk[k]], W=[r_vo[b]])
                sc.dma("pool", Vv[:, t * 4:(t + 1) * 4, :], vo[b][:], R=[r_vo[b]], ds=ds_o[b])
            sc.flush()
            ph.close()

        if want("C"):
            ph = Phase(nc, "c%d" % l)
            identb = ph.sb([128, 128], BF16, "identb")
            Tmul = ph.sb([128, H, 256], F32, "tmul")
            mask = ph.sb([128, 256], F32, "mask")
            b31 = ph.sb([128, H], F32, "b31")
            lv = ph.sb([128, 4, 64], F32, "lv")
            lprod = ph.sb([128, 2, 64], F32, "lprod")
            lsum = ph.sb([128, 8], F32, "lsum")
            sgs = ph.sb([128, 128], F32, "sgs")
            r_c, r_t, r_lam, r_sg, r_idb = Res(), Res(), Res(), Res(), Res()
            ds_c = sc.new_ds()
            sc.dma("pool", identb[:], c_ident, W=[r_idb], ds=sc.new_ds("pool"))
            sc.dma("sp", Tmul[:], btab, W=[r_t], ds=ds_c)
            sc.dma("sp", mask[:], c_mask, W=[r_c], ds=ds_c)
            sc.dma("sp", b31[:], relb[31:32, :].partition_broadcast(128), W=[r_c], ds=ds_c)
            for i in range(4):
                sc.dma("sp", lv[:, i, :], lamv[l, i:i + 1, :].partition_broadcast(128), W=[r_c], ds=ds_c)
            sc.dma("sp", sgs[:], subg[l:l + 1, :].partition_broadcast(128), W=[r_sg], ds=ds_c)
            sc.tt("dve", lprod[:, 0, :], lv[:, 0, :], lv[:, 1, :], ALU.mult, R=[r_c], W=[r_lam])
            sc.tt("dve", lprod[:, 1, :], lv[:, 2, :], lv[:, 3, :], ALU.mult, R=[r_c], W=[r_lam])
            for i in range(2):
                sc.op("dve", lambda h, o=lsum[:, i:i + 1], a=lprod[:, i, :]: h.reduce_sum(out=o, in_=a, axis=mybir.AxisListType.X),
                      R=[r_lam], W=[r_lam])
            sc.act(lsum[:, 2:4], lsum[:, 0:2], AF.Exp, R=[r_lam], W=[r_lam])
            sc.tt("dve", lsum[:, 5:6], lsum[:, 3:4], lsum[:, 2:3], ALU.subtract, R=[r_lam], W=[r_lam])
            sc.ts("dve", lsum[:, 4:5], lsum[:, 5:6], -lam_init, None, ALU.add, R=[r_lam], W=[r_lam])
            sc.ts("dve", sgs[:], sgs[:], 1.0 - lam_init, None, ALU.mult, R=[r_sg], W=[r_sg])
            for h in range(H):
                sc.ts("dve", Tmul[:, h, :], Tmul[:, h, :], b31[:, h:h + 1], None, ALU.subtract, R=[r_t, r_c], W=[r_t])
            for h in range(H):
                sc.act(Tmul[:, h, :], Tmul[:, h, :], AF.Exp, R=[r_t], W=[r_t])
            for h in range(H):
                sc.tt("dve", Tmul[:, h, :], Tmul[:, h, :], mask[:], ALU.mult, R=[r_t, r_c], W=[r_t])

            KTh = [ph.sb([128, S], BF16, "kth") for _ in range(2)]
            QTh = [ph.sb([128, S], BF16, "qth") for _ in range(2)]
            Vh = [ph.sb([128, 32, 129], BF16, "vh") for _ in range(2)]
            r_kq = [Res() for _ in range(2)]
            r_vh = [Res() for _ in range(2)]
            ds_h = [sc.new_ds() for _ in range(2)]
            for b in range(2):
                sc.memset("pool", Vh[b][:, :, 128:129], 1.0, W=[r_vh[b]])
            NPT = 4
            Tmul2 = ph.sb([128, H, 2, 256], F32, "tmul2")
            for h in range(H):
                for br in range(2):
                    sc.copy("pool", Tmul2[:, h, br, :], Tmul[:, h, :], R=[r_t], W=[r_t])
            Pt = [ph.sb([128, 2, 512], BF16, "pt") for _ in range(NPT)]
            r_pt = [Res() for _ in range(NPT)]
            Osb2 = [ph.sb([128, 9, 129], F32, "osb") for _ in range(2)]
            r_osb2 = [Res() for _ in range(2)]
            rl = ph.sb([128, 16], F32, "rl")
            r_rl = Res()
            ot4 = ph.sb([128, 4, 128], F32, "ot4")
            r_ot4 = Res()
            t4 = ph.sb([128, 4, 128], F32, "t4")
            r_t4 = Res()
            sq4 = ph.sb([128, 4, 128], F32, "sq4")
            r_sq4 = Res()
            ss = ph.sb([128, 16], F32, "ss")
            r_ss = Res()
            yb4 = ph.sb([128, 4, 128], BF16, "yb4")
            r_yb4 = Res()
            ybo = [ph.sb([128, TT], BF16, "ybo") for _ in range(2)]
            r_ybo = [Res() for _ in range(2)]
            ds_ybo = [sc.new_ds("pool") for _ in range(2)]
            Sb = [ph.ps([128, 2, 512], F32, "sb") for _ in range(2)]
            r_sb = [Res() for _ in range(2)]
            Ob = [ph.ps([128, 512], F32, "ob") for _ in range(3)]
            r_ob = [Res() for _ in range(3)]
            Tb = ph.ps([128, 512], BF16, "tb")
            r_tb = Res()

            def oreg(br, qb):
                i = br * 4 + qb
                return i // 3, (i % 3) * 129

            Vv = Vt.rearrange("(n p) c -> p n c", p=128)

            def load_head(h):
                b = h % 2
                sc.dma("sp", KTh[b][:], KT[h * 128:(h + 1) * 128, :], W=[r_kq[b]], ds=ds_h[b])
                sc.dma("sp", QTh[b][:], QT[h * 128:(h + 1) * 128, :], W=[r_kq[b]], ds=ds_h[b])
                sc.dma("sp", Vh[b][:, :, 0:128], Vv[:, :, h * 128:(h + 1) * 128], W=[r_vh[b]], ds=ds_h[b])

            load_head(0)
            pti = 0
            ci = 0
            pend = []
            items = [(h, c, j) for h in range(H) for c in range(NT) for j in range(4 * c + 4)]

            def emit_S(it):
                h, c, j = it
                hb_ = h % 2
                qlo = max(4 * c, j)
                off = (qlo - 4 * c) * 128
                for br in range(2):
                    sc.mm(Sb[j % 2][:, br, off:512],
                          KTh[hb_][br * 64:(br + 1) * 64, j * 128:(j + 1) * 128],
                          QTh[hb_][br * 64:(br + 1) * 64, c * 512 + off:(c + 1) * 512],
                          True, True, R=[r_kq[hb_]], W=[r_sb[j % 2]])

            emit_S(items[0])
            emit_S(items[1])
            touched = set()
            for ii, (h, c, j) in enumerate(items):
                hb_ = h % 2
                nj = 4 * c + 4
                if c == 0 and j == 0 and h + 1 < H:
                    load_head(h + 1)
                if j == 0:
                    touched = set()
                qlo = max(4 * c, j)
                off = (qlo - 4 * c) * 128
                p = pti % NPT
                pti += 1
                sc.act(Pt[p][:, :, off:512], Sb[j % 2][:, :, off:512], AF.Exp,
                       R=[r_sb[j % 2], r_c], W=[r_pt[p]], bias=b31[:, h:h + 1])
                q0 = max(j, 4 * c)
                q1 = min(j + 1, 4 * c + 3)
                if q0 <= q1:
                    a0 = (q0 - 4 * c) * 128
                    a1 = (q1 - 4 * c + 1) * 128
                    t0 = (q0 - j) * 128
                    t1 = (q1 - j + 1) * 128
                    sc.tt("dve", Pt[p][:, :, a0:a1], Pt[p][:, :, a0:a1], Tmul2[:, h, :, t0:t1], ALU.mult,
                          R=[r_t], W=[r_pt[p]])
                if ii + 2 < len(items):
                    emit_S(items[ii + 2])
                for br in range(2):
                    for qb in range(qlo, 4 * c + 4):
                        bk, co = oreg(br, qb - 4 * c)
                        first = bk not in touched
                        touched.add(bk)
                        sc.mm(Ob[bk][:, co:co + 129], Pt[p][:, br, (qb - 4 * c) * 128:(qb - 4 * c + 1) * 128],
                              Vh[hb_][:, j, :], first, False, R=[r_pt[p], r_vh[hb_]], W=[r_ob[bk]],
                              sig=(qb == 4 * c + 3), skip=True)
                if pend and j == (0, min(5, nj - 2), min(7, nj - 1))[3 - len(pend)]:
                    pend.pop(0)()
                if j < nj - 1:
                    continue
                ob_ = ci % 2
                ci += 1
                Osb = Osb2[ob_]
                r_osb = r_osb2[ob_]
                for bk in range(3):
                    n = 387 if bk < 2 else 258
                    sc.copy("dve", Osb[:, 3 * bk:3 * bk + n // 129, :], Ob[bk][:, 0:n].rearrange("p (a b) -> p a b", b=129),
                            R=[r_ob[bk]], W=[r_osb])

                def stage1(Osb=Osb, r_osb=r_osb):
                    sc.recip(rl[:, 0:8], Osb[:, 0:8, 128], R=[r_osb], W=[r_rl])
                    sc.tt("dve", ot4[:], Osb[:, 0:4, 0:128], rl[:, 0:4].unsqueeze(2).to_broadcast([128, 4, 128]), ALU.mult,
                          R=[r_osb, r_rl], W=[r_ot4])
                    sc.ts("dve", rl[:, 8:12], rl[:, 4:8], lsum[:, 4:5], None, ALU.mult, R=[r_rl, r_lam], W=[r_rl])
                    sc.tt("dve", t4[:], Osb[:, 4:8, 0:128], rl[:, 8:12].unsqueeze(2).to_broadcast([128, 4, 128]), ALU.mult,
                          R=[r_osb, r_rl], W=[r_t4])
                    sc.tt("dve", ot4[:], ot4[:], t4[:], ALU.add, R=[r_t4], W=[r_ot4])
                    sc.tt("dve", sq4[:], ot4[:], ot4[:], ALU.mult, R=[r_ot4], W=[r_sq4])
                    sc.op("dve", lambda hh, o=ss[:, 0:4], a=sq4[:]: hh.reduce_sum(out=o, in_=a, axis=mybir.AxisListType.X),
                          R=[r_sq4], W=[r_ss])

                def stage2():
                    sc.ts("dve", ss[:, 4:8], ss[:, 0:4], 1.0 / 128, EPS, ALU.mult, ALU.add, R=[r_ss], W=[r_ss])
                    sc.act(ss[:, 8:12], ss[:, 4:8], AF.Ln, R=[r_ss], W=[r_ss])
                    sc.act(ss[:, 12:16], ss[:, 8:12], AF.Exp, R=[r_ss], W=[r_ss], scale=-0.5)
                    sc.tt("pool", t4[:], ot4[:], ss[:, 12:16].unsqueeze(2).to_broadcast([128, 4, 128]), ALU.mult,
                          R=[r_ot4, r_ss], W=[r_t4])
                    sc.tt("pool", yb4[:], t4[:], sgs[:].unsqueeze(1).to_broadcast([128, 4, 128]), ALU.mult,
                          R=[r_t4, r_sg], W=[r_yb4])

                def stage3(h=h, c=c, ob_=ob_):
                    for qb in range(4):
                        sc.tr(Tb[:, qb * 128:(qb + 1) * 128], yb4[:, qb, :], identb[:], R=[r_yb4, r_idb], W=[r_tb], sig=(qb == 3))
                    sc.copy("dve", ybo[ob_][:], Tb[:], R=[r_tb], W=[r_ybo[ob_]])
                    sc.dma("pool", ybT[h * 128:(h + 1) * 128, c * TT:(c + 1) * TT], ybo[ob_][:], R=[r_ybo[ob_]], ds=ds_ybo[ob_])

                pend[:] = [stage1, stage2, stage3]
            for st_ in pend:
                st_()
            sc.flush()
            ph.close()

        if want("D"):
            ph = Phase(nc, "d%d" % l)
            Wb = ph.sb([128, 8, 1024], BF16, "wb")
            Wgb = ph.sb([128, 8, 1024], BF16, "wgb")
            Wo = ph.sb([128, 8, 1024], BF16, "wo")
            bnd = list(range(0, 1025, 256))
            wgb = WT(Wgb, w_gate[l][:, 1024:2048], bnd)
            wb = WT(Wb, w_b[l], bnd)
            wo = WT(Wo, w_out[l], bnd)
            for _ in range(4):
                wgb.issue(1)
                wb.issue(1)
            wo.issue()
            hb = [ph.sb([128, 8, TT], BF16, "hb") for _ in range(2)]
            yt = [ph.sb([128, 8, TT], BF16, "yt") for _ in range(2)]
            mt = [ph.sb([128, 8, TT], F32, "mt") for _ in range(2)]
            NX = 3
            xt = [ph.sb([128, 8, TT], F32, "xt") for _ in range(NX)]
            r_in = [Res() for _ in range(2)]
            r_yt = [Res() for _ in range(2)]
            r_mt = [Res() for _ in range(2)]
            r_xt = [Res() for _ in range(NX)]
            ds_in = [sc.new_ds() for _ in range(2)]
            ds_yt = [sc.new_ds() for _ in range(2)]
            ds_mt = [sc.new_ds() for _ in range(2)]
            ds_x = [sc.new_ds() for _ in range(NX)]
            ds_xo = [sc.new_ds("pool") for _ in range(NX)]
            gbt = [ph.sb([128, TT], F32, "gbt") for _ in range(2)]
            r_gb = [Res() for _ in range(2)]
            tmp = [ph.sb([128, TT], F32, "tmp") for _ in range(2)]
            r_tmp = [Res() for _ in range(2)]
            mrg = [ph.sb([128, 8, TT], BF16, "mrg") for _ in range(2)]
            r_mrg = [[Res() for _ in range(8)] for _ in range(2)]
            banks = [ph.ps([128, 512], F32, "bk") for _ in range(8)]
            r_bk = [Res() for _ in range(8)]
            bi = 0
            hTv, yTv, mTv, xTv = chunked(hT), chunked(ybT), chunked(m1T), chunked(xT)

            def loads(t):
                b = t % 2
                tok = slice(t * TT, (t + 1) * TT)
                sc.dma("sp", hb[b][:], hTv[:, :, tok], W=[r_in[b]], ds=ds_in[b])
                sc.dma("sp", yt[b][:], yTv[:, :, tok], W=[r_yt[b]], ds=ds_yt[b])
                sc.dma("sp", mt[b][:], mTv[:, :, tok], W=[r_mt[b]], ds=ds_mt[b])
                sc.dma("sp", xt[t % NX][:], xTv[:, :, tok], W=[r_xt[t % NX]], ds=ds_x[t % NX])

            def out_proj(t):
                nonlocal bi
                m = t % 2
                x = xt[t % NX]
                for e in range(8):
                    k = bi % 8
                    bi += 1
                    for kc in range(8):
                        sc.mm(banks[k][:], Wo[:, kc, e * 128:(e + 1) * 128], mrg[m][:, kc, :], kc == 0, kc == 7,
                              R=[wo.res(e * 128), r_mrg[m][kc]], W=[r_bk[k]])
                    sc.tt("dve", x[:, e, :], banks[k][:], x[:, e, :], ALU.add, R=[r_bk[k]], W=[r_xt[t % NX]])
                sc.dma("pool", xTv[:, :, t * TT:(t + 1) * TT], x[:], R=[r_xt[t % NX]], ds=ds_xo[t % NX])

            loads(0)
            ei = 0
            for t in range(NT):
                b = t % 2
                if t + 1 < NT:
                    loads(t + 1)
                for e in range(8):
                    eb = ei % 2
                    ei += 1
                    kB = bi % 8
                    bi += 1
                    for kc in range(8):
                        sc.mm(banks[kB][:], Wgb[:, kc, e * 128:(e + 1) * 128], hb[b][:, kc, :], kc == 0, kc == 7,
                              R=[wgb.res(e * 128), r_in[b]], W=[r_bk[kB]])
                    sc.act(gbt[eb][:], banks[kB][:], AF.Sigmoid, R=[r_bk[kB]], W=[r_gb[eb]])
                    kA = bi % 8
                    bi += 1
                    for kc in range(8):
                        sc.mm(banks[kA][:], Wb[:, kc, e * 128:(e + 1) * 128], yt[b][:, kc, :], kc == 0, kc == 7,
                              R=[wb.res(e * 128), r_yt[b]], W=[r_bk[kA]])
                    sc.tt("dve", tmp[eb][:], banks[kA][:], gbt[eb][:], ALU.mult, R=[r_bk[kA], r_gb[eb]], W=[r_tmp[eb]])
                    sc.tt("dve", mrg[b][:, e, :], tmp[eb][:], mt[b][:, e, :], ALU.add, R=[r_tmp[eb], r_mt[b]], W=[r_mrg[b][e]])
                if t >= 1:
                    out_proj(t - 1)
            out_proj(NT - 1)
            sc.flush()
            ph.close()

        if want("E"):
            ph = Phase(nc, "e%d" % l)
            NTE = S // TE
            Wup = ph.sb([128, 8, 2 * DFF], BF16, "wup")
            Wdn = ph.sb([128, NF, D], BF16, "wdn")
            wupa = WT(Wup, w_up[l], list(range(0, DFF + 1, 256)))
            wupb = WT(Wup, w_up[l], list(range(DFF, 2 * DFF + 1, 256)))
            wdn = WT(Wdn, w_down[l], list(range(0, 1025, 256)))
            cv = ph.sb([128, NF, 4], F32, "cv")
            g2 = ph.sb([128, 8], F32, "g2")
            ones_bf = ph.sb([128, 128], BF16, "ones")
            halo = ph.sb([128, NF, 2], F32, "halo")
            r_c, r_ones = Res(), Res()
            r_halo = [Res() for _ in range(NF)]
            ds_c = sc.new_ds()
            sc.memset("pool", ones_bf[:], 1.0, W=[r_ones])
            sc.memset("pool", halo[:], 0.0, W=r_halo)
            for _ in range(NF // 2):
                wupa.issue(1)
                wupb.issue(1)
            wdn.issue()
            NX = 3
            xt = [ph.sb([128, 8, TE], F32, "xt") for _ in range(NX)]
            r_xt = [Res() for _ in range(NX)]
            ds_xt = [sc.new_ds() for _ in range(NX)]
            ds_xo = [sc.new_ds("pool") for _ in range(NX)]
            sq = ph.sb([128, 8, TE], BF16, "sq")
            r_sq = Res()
            rstd = ph.sb([128, TE], F32, "rstd")
            r_rstd = Res()
            h2 = [ph.sb([128, 8, TE], BF16, "h2") for _ in range(2)]
            r_h2 = [Res() for _ in range(2)]
            ptmp = [ph.sb([128, TE], F32, "ptmp") for _ in range(2)]
            r_ptmp = [Res() for _ in range(2)]
            NA = 3
            asb = [ph.sb([128, TE + 2], F32, "asb") for _ in range(NA)]
            r_asb = [Res() for _ in range(NA)]
            t1 = [ph.sb([128, TE], F32, "t1") for _ in range(NA)]
            r_t1 = [Res() for _ in range(NA)]
            gl = [ph.sb([128, TE], F32, "gl") for _ in range(NA)]
            r_gl = [Res() for _ in range(NA)]
            gT = [ph.sb([128, NF, TE], BF16, "gT") for _ in range(2)]
            r_gT = [[Res() for _ in range(NF)] for _ in range(2)]
            banks = [ph.ps([128, 512], F32, "bk") for _ in range(8)]
            r_bk = [Res() for _ in range(8)]
            NUPB = 6
            st = {"bi": 0, "ai": 0, "di": 0}
            xTv = chunked(xT)

            def upbank():
                k = st["bi"] % NUPB
                st["bi"] += 1
                return k

            def pro_sq(t):
                x = xt[t % NX]
                for kc in range(8):
                    sc.act(sq[:, kc, :], x[:, kc, :], AF.Square, R=[r_xt[t % NX]], W=[r_sq])

            def pro_h(t):
                b = t % 2
                x = xt[t % NX]
                k = upbank()
                for kc in range(8):
                    sc.mm(banks[k][:, 0:TE], ones_bf[:], sq[:, kc, :], kc == 0, kc == 7, R=[r_sq, r_ones], W=[r_bk[k]])
                sc.ts("dve", rstd[:], banks[k][:, 0:TE], 1.0 / D, EPS, ALU.mult, ALU.add, R=[r_bk[k]], W=[r_rstd])
                sc.act(rstd[:], rstd[:], AF.Sqrt, R=[r_rstd], W=[r_rstd])
                sc.recip(rstd[:], rstd[:], R=[r_rstd], W=[r_rstd])
                if t == 0:
                    for kc in range(8):
                        pro_h2(t, kc)

            def pro_h2(t, kc):
                b = t % 2
                x = xt[t % NX]
                sc.stt("dve", h2[b][:, kc, :], x[:, kc, :], g2[:, kc:kc + 1], rstd[:], ALU.mult, ALU.mult,
                       R=[r_xt[t % NX], r_rstd, r_c], W=[r_h2[b]])

            def down_steps(t):
                b = t % 2
                x = xt[t % NX]
                rx = r_xt[t % NX]
                n = 0
                for e in range(8):
                    k = 6 + st["di"] % 2
                    st["di"] += 1
                    for f in range(NF):
                        sc.mm(banks[k][:, 0:TE], Wdn[:, f, e * 128:(e + 1) * 128], gT[b][:, f, :], f == 0, f == NF - 1,
                              R=[wdn.res(e * 128), r_gT[b][f]], W=[r_bk[k]])
                        n += 1
                        if f == NF - 1:
                            sc.tt("dve", x[:, e, :], banks[k][:, 0:TE], x[:, e, :], ALU.add, R=[r_bk[k]], W=[rx])
                            if e == 7:
                                sc.dma("pool", xTv[:, :, t * TE:(t + 1) * TE], x[:], R=[rx], ds=ds_xo[t % NX])
                        if n % 8 == 0:
                            yield

            def fin_f(t, f, a):
                b = t % 2
                kB = fin_f.kb[(t, f)]
                sc.act(gl[a][:], t1[a][:], AF.Gelu_apprx_tanh, R=[r_t1[a]], W=[r_gl[a]])
                sc.tt("dve", gT[b][:, f, :], banks[kB][:, 0:TE], gl[a][:], ALU.mult, R=[r_bk[kB], r_gl[a]], W=[r_gT[b][f]])

            fin_f.kb = {}

            sc.dma("sp", xt[0][:], xTv[:, :, 0:TE], W=[r_xt[0]], ds=ds_xt[0])
            sc.dma("sp", cv[:], convT[l], W=[r_c], ds=ds_c)
            sc.dma("sp", g2[:], g2T[l], W=[r_c], ds=ds_c)
            pro_sq(0)
            pro_h(0)
            dgen = None
            prev = None
            for t in range(NTE):
                b = t % 2
                if t + 1 < NTE:
                    n1 = (t + 1) % NX
                    sc.dma("sp", xt[n1][:], xTv[:, :, (t + 1) * TE:(t + 2) * TE], W=[r_xt[n1]], ds=ds_xt[n1])
                for f in range(NF):
                    if t + 1 < NTE and 5 <= f < 13:
                        sc.act(sq[:, f - 5, :], xt[(t + 1) % NX][:, f - 5, :], AF.Square, R=[r_xt[(t + 1) % NX]], W=[r_sq])
                    if t + 1 < NTE and f == 13:
                        pro_h(t + 1)
                    if t + 1 < NTE and 14 <= f < 22:
                        pro_h2(t + 1, f - 14)
                    a = st["ai"] % NA
                    st["ai"] += 1
                    kA = upbank()
                    for kc in range(8):
                        sc.mm(banks[kA][:, 0:TE], Wup[:, kc, f * 128:(f + 1) * 128], h2[b][:, kc, :], kc == 0, kc == 7,
                              R=[wupa.res(f * 128), r_h2[b]], W=[r_bk[kA]])
                    kB = upbank()
                    fin_f.kb[(t, f)] = kB
                    for kc in range(8):
                        sc.mm(banks[kB][:, 0:TE], Wup[:, kc, DFF + f * 128:DFF + (f + 1) * 128], h2[b][:, kc, :], kc == 0, kc == 7,
                              R=[wupb.res(DFF + f * 128), r_h2[b]], W=[r_bk[kB]])
                    if dgen is not None:
                        next(dgen, None)
                    sc.act(t1[a][:], banks[kA][:, 0:TE], AF.Identity, R=[r_bk[kA], r_c], W=[r_t1[a]],
                           scale=cv[:, f, 2:3], bias=cv[:, f, 3:4])
                    sc.copy("act", asb[a][:, 2:TE + 2], banks[kA][:, 0:TE], R=[r_bk[kA]], W=[r_asb[a]])
                    sc.copy("act", asb[a][:, 0:2], halo[:, f, :], R=[r_halo[f]], W=[r_asb[a]])
                    sc.copy("act", halo[:, f, :], asb[a][:, TE:TE + 2], R=[r_asb[a]], W=[r_halo[f]])
                    sc.stt("dve", t1[a][:], asb[a][:, 1:TE + 1], cv[:, f, 1:2], t1[a][:], ALU.mult, ALU.add,
                           R=[r_asb[a], r_c], W=[r_t1[a]])
                    sc.stt("dve", t1[a][:], asb[a][:, 0:TE], cv[:, f, 0:1], t1[a][:], ALU.mult, ALU.add,
                           R=[r_asb[a], r_c], W=[r_t1[a]])
                    if prev is not None:
                        fin_f(*prev)
                    prev = (t, f, a)
                fin_f(*prev)
                prev = None
                if dgen is not None:
                    for _ in dgen:
                        pass
                dgen = down_steps(t)
            for _ in dgen:
                pass
            sc.flush()
            ph.close()


    if want("F"):
        ph = Phase(nc, "f")
        ident = ph.sb([128, 128], F32, "ident")
        fg = ph.sb([128, D], F32, "fg")
        r_c = Res()
        ds_c = sc.new_ds()
        sc.dma("sp", ident[:], c_ident, W=[r_c], ds=ds_c)
        sc.dma("sp", fg[:], fing.partition_broadcast(128), W=[r_c], ds=ds_c)
        xi = [ph.sb([128, 8, 128], F32, "xi") for _ in range(3)]
        r_xi = [Res() for _ in range(3)]
        ds_xi = [sc.new_ds() for _ in range(3)]
        junk = ph.sb([128, 512], F32, "junk")
        r_junk = Res()
        ss = [ph.sb([128, 8], F32, "ss") for _ in range(2)]
        r_ss = [Res() for _ in range(2)]
        osb = [ph.sb([128, D], F32, "osb") for _ in range(2)]
        r_osb = [Res() for _ in range(2)]
        ds_o = [sc.new_ds("pool") for _ in range(2)]
        banks = [ph.ps([128, 512], F32, "bk") for _ in range(8)]
        r_bk = [Res() for _ in range(8)]
        xTv = chunked(xT)
        NB = S // 128
        sc.dma("sp", xi[0][:], xTv[:, :, 0:128], W=[r_xi[0]], ds=ds_xi[0])
        sc.dma("sp", xi[1][:], xTv[:, :, 128:256], W=[r_xi[1]], ds=ds_xi[1])
        for tb in range(NB):
            b3 = tb % 3
            b = tb % 2
            if tb + 2 < NB:
                n3 = (tb + 2) % 3
                sc.dma("sp", xi[n3][:], xTv[:, :, (tb + 2) * 128:(tb + 3) * 128], W=[r_xi[n3]], ds=ds_xi[n3])
            k0 = (tb * 2) % 8
            for hf in range(2):
                k = k0 + hf
                for c4 in range(4):
                    kc = hf * 4 + c4
                    sc.tr(banks[k][:, c4 * 128:(c4 + 1) * 128], xi[b3][:, kc, :], ident[:], R=[r_xi[b3], r_c], W=[r_bk[k]], sig=(c4 == 3))
                sc.act(junk[:], banks[k][:], AF.Square, R=[r_bk[k]], W=[r_junk, r_ss[b]], accum_out=ss[b][:, hf:hf + 1])
            sc.tt("dve", ss[b][:, 2:3], ss[b][:, 0:1], ss[b][:, 1:2], ALU.add, R=[r_ss[b]], W=[r_ss[b]])
            sc.ts("dve", ss[b][:, 3:4], ss[b][:, 2:3], 1.0 / D, EPS, ALU.mult, ALU.add, R=[r_ss[b]], W=[r_ss[b]])
            sc.act(ss[b][:, 4:5], ss[b][:, 3:4], AF.Sqrt, R=[r_ss[b]], W=[r_ss[b]])
            sc.recip(ss[b][:, 5:6], ss[b][:, 4:5], R=[r_ss[b]], W=[r_ss[b]])
            for hf in range(2):
                k = k0 + hf
                sc.stt("dve", osb[b][:, hf * 512:(hf + 1) * 512], banks[k][:], ss[b][:, 5:6], fg[:, hf * 512:(hf + 1) * 512],
                       ALU.mult, ALU.mult, R=[r_bk[k], r_ss[b], r_c], W=[r_osb[b]])
            sc.dma("pool", out[tb * 128:(tb + 1) * 128, :], osb[b][:], R=[r_osb[b]], ds=ds_o[b])
        sc.flush()
        ph.close()
    return nc


def _t5_bucket(rel):
    n = np.maximum(rel, 0)
    max_exact = 16
    nf = np.maximum(n, 1).astype(np.float32)
    large = max_exact + (np.log(nf / np.float32(max_exact)) / np.float32(math.log(128 / max_exact))
                         * np.float32(32 - max_exact)).astype(np.int32)
    large = np.minimum(large, 31)
    return np.where(n < max_exact, n, large)


def host_layout(inp):
    f = lambda a: np.ascontiguousarray(np.asarray(a, dtype=np.float32))
    shared = {}
    for k in ("w_in", "w_gate", "w_a", "w_b", "w_out", "w_up", "w_down", "rel_bias"):
        shared[k] = f(inp[k])
    shared["g1T"] = f(np.asarray(inp["norm1_g"]).reshape(L, 8, 128).transpose(0, 2, 1))
    shared["g2T"] = f(np.asarray(inp["norm2_g"]).reshape(L, 8, 128).transpose(0, 2, 1))
    shared["vng"] = f(inp["gmlp_vnorm_g"])
    shared["wsT"] = f(np.asarray(inp["gmlp_ws"]).transpose(0, 1, 3, 2))
    shared["gmlp_b"] = f(np.asarray(inp["gmlp_b"]).reshape(L, 1, D))
    shared["lamv"] = f(np.stack([np.asarray(inp["lam_q1"]), np.asarray(inp["lam_k1"]),
                                 np.asarray(inp["lam_q2"]), np.asarray(inp["lam_k2"])], axis=1))
    shared["subln_g"] = f(inp["subln_g"])
    cw = np.asarray(inp["conv_w"]).reshape(L, 3, NF, 128)
    cb = np.asarray(inp["conv_b"]).reshape(L, 1, NF, 128)
    shared["convT"] = f(np.concatenate([cw, cb], axis=1).transpose(0, 3, 2, 1))
    shared["final_g"] = f(np.asarray(inp["final_g"]).reshape(1, D))
    kk = np.arange(128)[:, None]
    cc = np.arange(256)[None, :]
    rel = cc - kk
    bucket = _t5_bucket(rel)
    shared["btab"] = f(np.asarray(inp["rel_bias"])[bucket].transpose(0, 2, 1))
    shared["c_mask"] = f((rel >= 0).astype(np.float32))
    shared["c_ident"] = f(np.eye(128))
    shared["c_tri"] = f((np.arange(128)[:, None] <= np.arange(128)[None, :]).astype(np.float32))
    return shared


_NC_CACHE = {}


def kernel(**inputs):
    shared = host_layout(inputs)
    x = np.asarray(inputs["x"], dtype=np.float32)
    if "nc" not in _NC_CACHE:
        _NC_CACHE["nc"] = build_program()
    nc = _NC_CACHE["nc"]
    in_maps = []
    for c in range(NCORES):
        m = dict(shared)
        m["x"] = np.ascontiguousarray(x[c])
        in_maps.append(m)
    res = run_bass_kernel_spmd(nc, in_maps, core_ids=list(range(NCORES)))
    return np.stack([np.asarray(r["out"]) for r in res.results], axis=0).astype(np.float32)
```

```python
import math
from contextlib import ExitStack

import numpy as np
import concourse.bass as bass
import concourse.mybir as mybir
from concourse.bass_utils import run_bass_kernel_spmd

F32 = mybir.dt.float32
BF16 = mybir.dt.bfloat16
AF = mybir.ActivationFunctionType
ALU = mybir.AluOpType

NCORES = 8
S = 4096
D = 1024
L = 2
DFF = 2816
NF = DFF // 128
H = 8
EPS = 1e-6
TT = 512
NT = S // TT
TE = 256
ENG = ("pe", "act", "dve", "pool", "sp")
NDS_SP = 28
NDS_POOL = 64


class Ev:
    __slots__ = ("sem", "val", "ds")

    def __init__(self, sem, val, ds=None):
        self.sem = sem
        self.val = val
        self.ds = ds


class Res:
    __slots__ = ("name", "w", "r")

    def __init__(self, name=""):
        self.name = name
        self.w = None
        self.r = {}


class DSem:
    def __init__(self, sem, q):
        self.sem = sem
        self.q = q
        self.cnt = 0
        self.drained = 0


class Sched:
    def __init__(self, nc):
        self.nc = nc
        self.sem = {e: nc.alloc_semaphore(name="sem_" + e) for e in ENG}
        self.cnt = {e: 0 for e in ENG}
        self.waited = {e: {} for e in ENG}
        self.stream = {e: [] for e in ENG}
        self.pe_pending = []
        self.dsq = {"sp": [DSem(nc.alloc_semaphore(name="dsp%d" % i), "sp") for i in range(NDS_SP)],
                    "pool": [DSem(nc.alloc_semaphore(name="dpl%d" % i), "pool") for i in range(NDS_POOL)]}
        self.dsems = self.dsq["sp"] + self.dsq["pool"]
        self.ds_next = {"sp": 0, "pool": 0}

    def new_ds(self, q="sp"):
        lst = self.dsq[q]
        ds = lst[self.ds_next[q] % len(lst)]
        self.ds_next[q] += 1
        return ds

    def _wait(self, e, ev):
        if ev is None:
            return
        if e == "pe" and ev.sem is self.sem["pe"]:
            return
        val = ev.ds.cnt if ev.ds is not None else ev.val
        assert val is not None, "waiting on an unsignaled PE instruction"
        w = self.waited[e]
        key = ev.sem.num
        if w.get(key, 0) >= val:
            return
        w[key] = val
        self.stream[e].append(("w", ev.sem, val))

    def _deps(self, e, R, W):
        for r in R:
            self._wait(e, r.w)
        for w in W:
            self._wait(e, w.w)
            for ev in w.r.values():
                self._wait(e, ev)

    def _commit(self, ev, R, W):
        for r in R:
            r.r[ev.sem.num] = ev
        for w in W:
            w.w = ev
            w.r = {}

    def op(self, e, fn, R=(), W=(), sig=True):
        self._deps(e, R, W)
        ev = Ev(self.sem[e], None)
        if sig:
            self.cnt[e] += 1
            ev.val = self.cnt[e]
            if e == "pe":
                for p in self.pe_pending:
                    p.val = ev.val
                self.pe_pending = []
            self.stream[e].append(("o", fn, self.sem[e]))
        else:
            assert e == "pe"
            self.pe_pending.append(ev)
            self.stream[e].append(("o", fn, None))
        self._commit(ev, R, W)

    def dma(self, q, out, in_, R=(), W=(), ds=None):
        assert ds.q == q, (ds.q, q)
        self._deps(q, R, W)
        ds.cnt += 16
        ev = Ev(ds.sem, ds.cnt, ds)
        self.stream[q].append(("d", out, in_, ds.sem))
        self._commit(ev, R, W)

    def mm(self, out, lhsT, rhs, start, stop, R=(), W=(), sig=None, skip=False):
        if sig is None:
            sig = stop
        if skip:
            fn = lambda h, o=out, l=lhsT, r=rhs, s=start, t=stop: h.matmul(o, l, r, start=s, stop=t, skip_group_check=True)
        else:
            fn = lambda h, o=out, l=lhsT, r=rhs, s=start, t=stop: h.matmul(o, l, r, start=s, stop=t)
        self.op("pe", fn, R, W, sig)

    def tr(self, out, in_, ident, R=(), W=(), sig=True):
        self.op("pe", lambda h, o=out, i=in_, d=ident: h.transpose(o, i, d), R, W, sig)

    def act(self, out, in_, func, R=(), W=(), **kw):
        self.op("act", lambda h, o=out, i=in_, f=func, k=kw: h.activation(out=o, in_=i, func=f, **k), R, W)

    def tt(self, e, out, in0, in1, op, R=(), W=()):
        self.op(e, lambda h, o=out, a=in0, b=in1, p=op: h.tensor_tensor(out=o, in0=a, in1=b, op=p), R, W)

    def ts(self, e, out, in0, s1, s2, op0, op1=None, R=(), W=()):
        if op1 is None:
            fn = lambda h, o=out, a=in0, x=s1, p=op0: h.tensor_scalar(out=o, in0=a, scalar1=x, scalar2=None, op0=p)
        else:
            fn = lambda h, o=out, a=in0, x=s1, y=s2, p=op0, q=op1: h.tensor_scalar(out=o, in0=a, scalar1=x, scalar2=y, op0=p, op1=q)
        self.op(e, fn, R, W)

    def stt(self, e, out, in0, scalar, in1, op0, op1, R=(), W=()):
        self.op(e, lambda h, o=out, a=in0, s=scalar, b=in1, p=op0, q=op1: h.scalar_tensor_tensor(out=o, in0=a, scalar=s, in1=b, op0=p, op1=q), R, W)

    def copy(self, e, out, in_, R=(), W=()):
        if e == "act":
            self.op(e, lambda h, o=out, i=in_: h.activation(out=o, in_=i, func=AF.Copy), R, W)
        else:
            self.op(e, lambda h, o=out, i=in_: h.tensor_copy(out=o, in_=i), R, W)

    def memset(self, e, ap, val, R=(), W=()):
        self.op(e, lambda h, a=ap, v=val: h.memset(a, v), R, W)

    def recip(self, out, in_, R=(), W=()):
        self.op("dve", lambda h, o=out, i=in_: h.reciprocal(out=o, in_=i), R, W)

    def flush(self):
        for ds in self.dsems:
            if ds.cnt > ds.drained:
                self.stream["sp"].append(("w", ds.sem, ds.cnt))
                ds.drained = ds.cnt
        streams = self.stream
        self.stream = {e: [] for e in ENG}

        def run(h, items):
            for it in items:
                if it[0] == "w":
                    h.wait_ge(it[1], it[2])
                elif it[0] == "o":
                    ins = it[1](h)
                    if it[2] is not None:
                        ins.then_inc(it[2], 1)
                else:
                    h.dma_start(out=it[1], in_=it[2]).then_inc(it[3], 16)

        with self.nc.Block() as block:
            @block.tensor
            def _(h):
                run(h, streams["pe"])

            @block.scalar
            def _(h):
                run(h, streams["act"])

            @block.vector
            def _(h):
                run(h, streams["dve"])

            @block.gpsimd
            def _(h):
                run(h, streams["pool"])

            @block.sync
            def _(h):
                run(h, streams["sp"])


class Phase:
    def __init__(self, nc, tag):
        self.nc = nc
        self.tag = tag
        self.es = ExitStack()
        self.n = 0

    def sb(self, shape, dt, name=None):
        self.n += 1
        return self.es.enter_context(self.nc.sbuf_tensor("%s_%s%d" % (self.tag, name or "t", self.n), shape, dt))

    def ps(self, shape, dt, name=None):
        self.n += 1
        return self.es.enter_context(self.nc.psum_tensor("%s_%s%d" % (self.tag, name or "p", self.n), shape, dt))

    def close(self):
        self.es.close()


def chunked(ap2d):
    return ap2d.rearrange("(kc p) t -> p kc t", p=128)


def build_program(dbg=False, phases=None, nlayers=L):
    nc = bass.Bass("TRN2", target_bir_lowering=False)
    sc = Sched(nc)

    def din(name, shape, dt=F32):
        return nc.dram_tensor(name, shape, dt, kind="ExternalInput").ap()

    def dscr(name, shape, dt):
        if dbg:
            return nc.dram_tensor(name, shape, dt, kind="ExternalOutput").ap()
        return nc.dram_tensor(name, shape, dt).ap()

    x_in = din("x", [S, D])
    w_in = din("w_in", [L, D, 5120])
    w_gate = din("w_gate", [L, D, 2048])
    w_a = din("w_a", [L, D, D])
    w_b = din("w_b", [L, D, D])
    w_out = din("w_out", [L, D, D])
    w_up = din("w_up", [L, D, 2 * DFF])
    w_down = din("w_down", [L, DFF, D])
    g1T = din("g1T", [L, 128, 8])
    g2T = din("g2T", [L, 128, 8])
    vng = din("vng", [L, D])
    wsT = din("wsT", [L, 8, 128, 128])
    gb = din("gmlp_b", [L, 1, D])
    lamv = din("lamv", [L, 4, 64])
    subg = din("subln_g", [L, 128])
    relb = din("rel_bias", [32, H])
    btab = din("btab", [128, H, 256])
    convT = din("convT", [L, 128, NF, 4])
    fing = din("final_g", [1, D])
    c_ident = din("c_ident", [128, 128])
    c_tri = din("c_tri", [128, 128])
    c_mask = din("c_mask", [128, 256])
    out = nc.dram_tensor("out", [S, D], F32, kind="ExternalOutput").ap()

    xT = dscr("xT", [D, S], F32)
    hT = dscr("hT", [D, S], BF16)
    m1T = dscr("m1T", [D, S], F32)
    QT = dscr("QT", [D, S], BF16)
    KT = dscr("KT", [D, S], BF16)
    Vt = dscr("Vt", [S, D], BF16)
    ybT = dscr("ybT", [D, S], BF16)

    def want(p):
        return phases is None or p in phases

    class WT:
        def __init__(self, dst, src2d, bounds):
            self.t = dst
            self.v = src2d.rearrange("(kc p) e -> p kc e", p=128)
            self.b = bounds
            self.r = [Res() for _ in range(len(bounds) - 1)]
            self.issued = 0

        def issue(self, n=None):
            n = len(self.r) - self.issued if n is None else n
            for _ in range(n):
                i = self.issued
                c0, c1 = self.b[i], self.b[i + 1]
                sc.dma("pool", self.t[:, :, c0:c1], self.v[:, :, c0:c1], W=[self.r[i]], ds=sc.new_ds("pool"))
                self.issued += 1

        def res(self, c):
            for i in range(len(self.r)):
                if self.b[i] <= c < self.b[i + 1]:
                    return self.r[i]
            raise KeyError(c)

    def load_w(ph, dst, src2d, res, ds, kc_n, q="pool"):
        v = src2d.rearrange("(kc p) e -> p kc e", p=128)
        for kc in range(kc_n):
            sc.dma(q, dst[:, kc, :], v[:, kc, :], W=[res], ds=ds)

    def rms_stats(ph, xt, xt_res, sq, sq_res, bank, bank_res, rstd, rstd_res, ones_bf, ones_res, ncol, nk=8):
        for kc in range(nk):
            sc.act(sq[:, kc, :], xt[:, kc, :], AF.Square, R=[xt_res], W=[sq_res])
        for kc in range(nk):
            sc.mm(bank, ones_bf[:], sq[:, kc, :], kc == 0, kc == nk - 1, R=[sq_res, ones_res], W=[bank_res])
        sc.ts("dve", rstd, bank, 1.0 / D, EPS, ALU.mult, ALU.add, R=[bank_res], W=[rstd_res])
        sc.act(rstd, rstd, AF.Sqrt, R=[rstd_res], W=[rstd_res])
        sc.recip(rstd, rstd, R=[rstd_res], W=[rstd_res])

    if want("T0"):
        ph = Phase(nc, "t0")
        ident = ph.sb([128, 128], F32, "ident")
        r_ident = Res()
        ds_c = sc.new_ds()
        sc.dma("sp", ident[:], c_ident, W=[r_ident], ds=ds_c)
        NB3 = 3
        xin = [ph.sb([128, 4, D], F32, "xin") for _ in range(NB3)]
        r_xin = [Res() for _ in range(NB3)]
        ds_xin = [sc.new_ds() for _ in range(NB3)]
        xo = [ph.sb([128, 8, TT], F32, "xo") for _ in range(NB3)]
        r_xo = [Res() for _ in range(NB3)]
        ds_xo = [sc.new_ds("pool") for _ in range(NB3)]
        banks = [ph.ps([128, 512], F32, "bk") for _ in range(8)]
        r_banks = [Res() for _ in range(8)]
        xv = x_in.rearrange("(n s p) d -> n p s d", p=128, s=4)
        xTv = chunked(xT)
        bi = 0
        sc.dma("sp", xin[0][:], xv[0], W=[r_xin[0]], ds=ds_xin[0])
        sc.dma("sp", xin[1][:], xv[1], W=[r_xin[1]], ds=ds_xin[1])
        for t in range(NT):
            b = t % NB3
            if t + 2 < NT:
                n2 = (t + 2) % NB3
                sc.dma("sp", xin[n2][:], xv[t + 2], W=[r_xin[n2]], ds=ds_xin[n2])
            for kc in range(8):
                bk = bi % 8
                bi += 1
                for s_ in range(4):
                    sc.tr(banks[bk][:, s_ * 128:(s_ + 1) * 128], xin[b][:, s_, kc * 128:(kc + 1) * 128], ident[:],
                          R=[r_xin[b], r_ident], W=[r_banks[bk]], sig=(s_ == 3))
                sc.copy("act" if kc % 2 == 0 else "dve", xo[b][:, kc, :], banks[bk][:], R=[r_banks[bk]], W=[r_xo[b]])
            sc.dma("pool", xTv[:, :, t * TT:(t + 1) * TT], xo[b][:], R=[r_xo[b]], ds=ds_xo[b])
        sc.flush()
        ph.close()

    for l in range(nlayers):
        lam_init = 0.8 - 0.6 * math.exp(-0.3 * l)
        if want("A"):
            ph = Phase(nc, "a%d" % l)
            Wuv = ph.sb([128, 8, 2048], BF16, "wuv")
            Wga = ph.sb([128, 8, 1024], BF16, "wga")
            Wa = ph.sb([128, 8, 1024], BF16, "wa")
            Wm32 = ph.sb([128, 8, 128], F32, "wm32")
            tri = ph.sb([128, 128], F32, "tri")
            Wm = ph.sb([128, 8, 128], BF16, "wm")
            bias_bc = ph.sb([128, 8, 128], F32, "biasbc")
            mtmp = [ph.sb([128, 4, 128], F32, "mtmp") for _ in range(2)]
            r_mtmp = [Res() for _ in range(2)]
            ones_bf = ph.sb([128, 128], BF16, "ones")
            vng_bc = ph.sb([128, D], F32, "vng")
            g1 = ph.sb([128, 8], F32, "g1")
            r_wm, r_c, r_b, r_ones = Res(), Res(), Res(), Res()
            ds_c = sc.new_ds()
            xt = [ph.sb([128, 8, TT], F32, "xt") for _ in range(2)]
            r_xt = [Res() for _ in range(2)]
            ds_xt = [sc.new_ds() for _ in range(2)]
            xTv = chunked(xT)
            hTv = chunked(hT)
            sc.dma("sp", xt[0][:], xTv[:, :, 0:TT], W=[r_xt[0]], ds=ds_xt[0])
            sc.dma("sp", g1[:], g1T[l], W=[r_c], ds=ds_c)
            sc.dma("sp", vng_bc[:], vng[l:l + 1, :].partition_broadcast(128), W=[r_c], ds=ds_c)
            sc.dma("sp", tri[:], c_tri, W=[r_c], ds=ds_c)
            sc.dma("sp", Wm32[:], wsT[l].rearrange("g j i -> j g i"), W=[r_c], ds=ds_c)
            sc.dma("sp", bias_bc[:], gb[l].partition_broadcast(128).rearrange("p o (g i) -> p (o g) i", g=8), W=[r_c], ds=ds_c)
            sc.memset("pool", ones_bf[:], 1.0, W=[r_ones])
            wuv = WT(Wuv, w_in[l][:, 0:2048], [0, 256, 512, 768, 1024, 1536, 2048])
            wga = WT(Wga, w_gate[l][:, 0:1024], [0, 256, 512, 768, 1024])
            wa = WT(Wa, w_a[l], [0, 256, 512, 768, 1024])
            wuv.issue()
            wga.issue()
            wa.issue()
            for g in range(8):
                sc.tt("dve", Wm[:, g, :], Wm32[:, g, :], tri[:], ALU.mult, R=[r_c], W=[r_wm])

            sq = ph.sb([128, 8, TT], BF16, "sq")
            r_sq = Res()
            rstd = ph.sb([128, TT], F32, "rstd")
            r_rstd = Res()
            hTt = [ph.sb([128, 8, TT], BF16, "hT") for _ in range(2)]
            r_hT = [Res() for _ in range(2)]
            ds_hT = [sc.new_ds("pool") for _ in range(2)]
            guT = ph.sb([128, 8, TT], F32, "guT")
            r_gu = [Res() for _ in range(8)]
            gv = [ph.sb([128, D], F32, "gv") for _ in range(2)]
            r_gv = [Res() for _ in range(2)]
            junk = ph.sb([128, D], BF16, "junk")
            r_junk = Res()
            ssv = ph.sb([128, 8], F32, "ssv")
            r_ssv = [Res() for _ in range(2)]
            vn = ph.sb([128, 4, D], BF16, "vn")
            r_vn = [Res() for _ in range(4)]
            yaT = ph.sb([128, 8, TT], BF16, "yaT")
            r_ya = [Res() for _ in range(8)]
            gaT = ph.sb([128, 8, TT], F32, "gaT")
            r_ga = [Res() for _ in range(8)]
            m1 = [ph.sb([128, TT], F32, "m1") for _ in range(2)]
            r_m1 = [Res() for _ in range(2)]
            ds_m1 = [sc.new_ds("pool") for _ in range(2)]
            banks = [ph.ps([128, 512], F32, "bk") for _ in range(8)]
            r_bk = [Res() for _ in range(8)]
            bi = [0]

            def nb():
                k = bi[0] % 8
                bi[0] += 1
                return k

            def pro_sq(t):
                b = t % 2
                for kc in range(8):
                    sc.act(sq[:, kc, :], xt[b][:, kc, :], AF.Square, R=[r_xt[b]], W=[r_sq])

            def pro_stats(t):
                k = nb()
                for kc in range(8):
                    sc.mm(banks[k][:], ones_bf[:], sq[:, kc, :], kc == 0, kc == 7, R=[r_sq, r_ones], W=[r_bk[k]])
                sc.ts("dve", rstd[:], banks[k][:], 1.0 / D, EPS, ALU.mult, ALU.add, R=[r_bk[k]], W=[r_rstd])
                sc.act(rstd[:], rstd[:], AF.Sqrt, R=[r_rstd], W=[r_rstd])
                sc.recip(rstd[:], rstd[:], R=[r_rstd], W=[r_rstd])

            def pro_h1(t, kc):
                b = t % 2
                sc.stt("dve", hTt[b][:, kc, :], xt[b][:, kc, :], g1[:, kc:kc + 1], rstd[:], ALU.mult, ALU.mult,
                       R=[r_xt[b], r_rstd, r_c], W=[r_hT[b]])
                if kc == 7:
                    sc.dma("pool", hTv[:, :, t * TT:(t + 1) * TT], hTt[b][:], R=[r_hT[b]], ds=ds_hT[b])

            def pro_h(t):
                pro_stats(t)
                for kc in range(8):
                    pro_h1(t, kc)

            pro_sq(0)
            pro_h(0)
            ei = 0
            for t in range(NT):
                b = t % 2
                tok = slice(t * TT, (t + 1) * TT)
                hc = hTt[b]
                if t + 1 < NT:
                    sc.dma("sp", xt[1 - b][:], xTv[:, :, (t + 1) * TT:(t + 2) * TT], W=[r_xt[1 - b]], ds=ds_xt[1 - b])
                for e in range(8):
                    k = nb()
                    for kc in range(8):
                        sc.mm(banks[k][:], Wuv[:, kc, e * 128:(e + 1) * 128], hc[:, kc, :], kc == 0, kc == 7,
                              R=[wuv.res(e * 128), r_hT[b]], W=[r_bk[k]])
                    sc.act(guT[:, e, :], banks[k][:], AF.Gelu_apprx_tanh, R=[r_bk[k]], W=[r_gu[e]])
                for s in range(4):
                    gs = s % 2
                    for half in range(2):
                        k = nb()
                        for kc in range(8):
                            sc.mm(banks[k][:], hc[:, kc, s * 128:(s + 1) * 128],
                                  Wuv[:, kc, 1024 + half * 512:1024 + (half + 1) * 512], kc == 0, kc == 7,
                                  R=[wuv.res(1024 + half * 512), r_hT[b]], W=[r_bk[k]])
                        sc.act(gv[gs][:, half * 512:(half + 1) * 512], banks[k][:], AF.Gelu_apprx_tanh,
                               R=[r_bk[k]], W=[r_gv[gs]])
                        if t + 1 < NT:
                            kq = s * 2 + half
                            sc.act(sq[:, kq, :], xt[1 - b][:, kq, :], AF.Square, R=[r_xt[1 - b]], W=[r_sq])
                    sc.act(junk[:], gv[gs][:], AF.Square, R=[r_gv[gs]], W=[r_junk, r_ssv[gs]],
                           accum_out=ssv[:, gs * 4:gs * 4 + 1])
                    sc.ts("dve", ssv[:, gs * 4 + 1:gs * 4 + 2], ssv[:, gs * 4:gs * 4 + 1], 1.0 / D, EPS, ALU.mult, ALU.add,
                          R=[r_ssv[gs]], W=[r_ssv[gs]])
                    sc.act(ssv[:, gs * 4 + 2:gs * 4 + 3], ssv[:, gs * 4 + 1:gs * 4 + 2], AF.Sqrt, R=[r_ssv[gs]], W=[r_ssv[gs]])
                    sc.recip(ssv[:, gs * 4 + 3:gs * 4 + 4], ssv[:, gs * 4 + 2:gs * 4 + 3], R=[r_ssv[gs]], W=[r_ssv[gs]])
                    sc.stt("dve", vn[:, s, :], gv[gs][:], ssv[:, gs * 4 + 3:gs * 4 + 4], vng_bc[:], ALU.mult, ALU.mult,
                           R=[r_gv[gs], r_ssv[gs], r_c], W=[r_vn[s]])
                if t + 1 < NT:
                    pro_stats(t + 1)
                def gates(e0, e1):
                    for e in range(e0, e1):
                        kB = nb()
                        for kc in range(8):
                            sc.mm(banks[kB][:], Wga[:, kc, e * 128:(e + 1) * 128], hc[:, kc, :], kc == 0, kc == 7,
                                  R=[wga.res(e * 128), r_hT[b]], W=[r_bk[kB]])
                        sc.act(gaT[:, e, :], banks[kB][:], AF.Sigmoid, R=[r_bk[kB]], W=[r_ga[e]])
                        if t + 1 < NT:
                            pro_h1(t + 1, e)

                gates(0, 4)
                for g in range(8):
                    k = nb()
                    for s in range(4):
                        o = banks[k][:, s * 128:(s + 1) * 128]
                        sc.mm(o, vn[:, s, g * 128:(g + 1) * 128], Wm[:, g, :], True, True, R=[r_vn[s], r_wm], W=[r_bk[k]], sig=(s == 3))
                    mb = g % 2
                    sc.tt("dve", mtmp[mb][:], banks[k][:].rearrange("p (s i) -> p s i", i=128),
                          bias_bc[:, g, :].unsqueeze(1).to_broadcast([128, 4, 128]), ALU.add,
                          R=[r_bk[k], r_c], W=[r_mtmp[mb]])
                    sc.tt("dve", yaT[:, g, :], mtmp[mb][:].rearrange("p s i -> p (s i)"), guT[:, g, :], ALU.mult,
                          R=[r_mtmp[mb], r_gu[g]], W=[r_ya[g]])
                gates(4, 8)
                for e in range(8):
                    eb = ei % 2
                    ei += 1
                    kA = nb()
                    for g in range(8):
                        sc.mm(banks[kA][:], Wa[:, g, e * 128:(e + 1) * 128], yaT[:, g, :], g == 0, g == 7,
                              R=[wa.res(e * 128), r_ya[g]], W=[r_bk[kA]])
                    sc.tt("dve", m1[eb][:], banks[kA][:], gaT[:, e, :], ALU.mult, R=[r_bk[kA], r_ga[e]], W=[r_m1[eb]])
                    sc.dma("pool", m1T[e * 128:(e + 1) * 128, tok], m1[eb][:], R=[r_m1[eb]], ds=ds_m1[eb])
            sc.flush()
            ph.close()


        if want("B"):
            ph = Phase(nc, "b%d" % l)
            Wq = ph.sb([128, 8, 3072], BF16, "wqkv")
            wq = WT(Wq, w_in[l][:, 2048:5120], list(range(0, 2048, 256)) + [2048, 2560, 3072])
            wq.issue()
            hb = [ph.sb([128, 8, TT], BF16, "hb") for _ in range(2)]
            r_hb = [Res() for _ in range(2)]
            ds_hb = [sc.new_ds() for _ in range(2)]
            qo = [ph.sb([128, 8, TT], BF16, "qo") for _ in range(2)]
            ko = [ph.sb([128, 8, TT], BF16, "ko") for _ in range(2)]
            vo = [ph.sb([128, 4, D], BF16, "vo") for _ in range(2)]
            r_qo = [Res() for _ in range(2)]
            r_ko = [Res() for _ in range(2)]
            r_vo = [Res() for _ in range(2)]
            ds_o = [sc.new_ds("pool") for _ in range(2)]
            banks = [ph.ps([128, 512], F32, "bk") for _ in range(8)]
            r_bk = [Res() for _ in range(8)]
            bi = 0
            hTv = chunked(hT)
            QTv = chunked(QT)
            KTv = chunked(KT)
            Vv = Vt.rearrange("(n p) c -> p n c", p=128)
            sc.dma("sp", hb[0][:], hTv[:, :, 0:TT], W=[r_hb[0]], ds=ds_hb[0])
            for t in range(NT):
                b = t % 2
                tok = slice(t * TT, (t + 1) * TT)
                if t + 1 < NT:
                    sc.dma("sp", hb[1 - b][:], hTv[:, :, (t + 1) * TT:(t + 2) * TT], W=[r_hb[1 - b]], ds=ds_hb[1 - b])
                for e in range(16):
                    k = bi % 8
                    bi += 1
                    for kc in range(8):
                        sc.mm(banks[k][:], Wq[:, kc, e * 128:(e + 1) * 128], hb[b][:, kc, :], kc == 0, kc == 7,
                              R=[wq.res(e * 128), r_hb[b]], W=[r_bk[k]])
                    if e < 8:
                        sc.act(qo[b][:, e, :], banks[k][:], AF.Copy, R=[r_bk[k]], W=[r_qo[b]], scale=0.125)
                    else:
                        sc.copy("dve", ko[b][:, e - 8, :], banks[k][:], R=[r_bk[k]], W=[r_ko[b]])
                sc.dma("pool", QTv[:, :, tok], qo[b][:], R=[r_qo[b]], ds=ds_o[b])
                sc.dma("pool", KTv[:, :, tok], ko[b][:], R=[r_ko[b]], ds=ds_o[b])
                for s in range(4):
                    for half in range(2):
                        k = bi % 8
                        bi += 1
                        for kc in range(8):
                            sc.mm(banks[k][:], hb[b][:, kc, s * 128:(s + 1) * 128],
                                  Wq[:, kc, 2048 + half * 512:2048 + (half + 1) * 512], kc == 0, kc == 7,
                                  R=[wq.res(2048 + half * 512), r_hb[b]], W=[r_bk[k]])
                        sc.copy("act" if half == 0 else "dve", vo[b][:, s, half * 512:(half + 1) * 512], banks[k][:],
                                R=[r_bk[k]], W=[r_vo[b]])
                sc.dma("pool", Vv[:, t * 4:(t + 1) * 4, :], vo[b][:], R=[r_vo[b]], ds=ds_o[b])
            sc.flush()
            ph.close()

        if want("C"):
            ph = Phase(nc, "c%d" % l)
            identb = ph.sb([128, 128], BF16, "identb")
            Tmul = ph.sb([128, H, 256], F32, "tmul")
            mask = ph.sb([128, 256], F32, "mask")
            b31 = ph.sb([128, H], F32, "b31")
            lv = ph.sb([128, 4, 64], F32, "lv")
            lprod = ph.sb([128, 2, 64], F32, "lprod")
            lsum = ph.sb([128, 8], F32, "lsum")
            sgs = ph.sb([128, 128], F32, "sgs")
            r_c, r_t, r_lam, r_sg, r_idb = Res(), Res(), Res(), Res(), Res()
            ds_c = sc.new_ds()
            sc.dma("pool", identb[:], c_ident, W=[r_idb], ds=sc.new_ds("pool"))
            sc.dma("sp", Tmul[:], btab, W=[r_t], ds=ds_c)
            sc.dma("sp", mask[:], c_mask, W=[r_c], ds=ds_c)
            sc.dma("sp", b31[:], relb[31:32, :].partition_broadcast(128), W=[r_c], ds=ds_c)
            for i in range(4):
                sc.dma("sp", lv[:, i, :], lamv[l, i:i + 1, :].partition_broadcast(128), W=[r_c], ds=ds_c)
            sc.dma("sp", sgs[:], subg[l:l + 1, :].partition_broadcast(128), W=[r_sg], ds=ds_c)
            sc.tt("dve", lprod[:, 0, :], lv[:, 0, :], lv[:, 1, :], ALU.mult, R=[r_c], W=[r_lam])
            sc.tt("dve", lprod[:, 1, :], lv[:, 2, :], lv[:, 3, :], ALU.mult, R=[r_c], W=[r_lam])
            for i in range(2):
                sc.op("dve", lambda h, o=lsum[:, i:i + 1], a=lprod[:, i, :]: h.reduce_sum(out=o, in_=a, axis=mybir.AxisListType.X),
                      R=[r_lam], W=[r_lam])
            sc.act(lsum[:, 2:4], lsum[:, 0:2], AF.Exp, R=[r_lam], W=[r_lam])
            sc.tt("dve", lsum[:, 5:6], lsum[:, 3:4], lsum[:, 2:3], ALU.subtract, R=[r_lam], W=[r_lam])
            sc.ts("dve", lsum[:, 4:5], lsum[:, 5:6], -lam_init, None, ALU.add, R=[r_lam], W=[r_lam])
            sc.ts("dve", sgs[:], sgs[:], 1.0 - lam_init, None, ALU.mult, R=[r_sg], W=[r_sg])
            for h in range(H):
                sc.ts("dve", Tmul[:, h, :], Tmul[:, h, :], b31[:, h:h + 1], None, ALU.subtract, R=[r_t, r_c], W=[r_t])
            for h in range(H):
                sc.act(Tmul[:, h, :], Tmul[:, h, :], AF.Exp, R=[r_t], W=[r_t])
            for h in range(H):
                sc.tt("dve", Tmul[:, h, :], Tmul[:, h, :], mask[:], ALU.mult, R=[r_t, r_c], W=[r_t])

            KTh = [ph.sb([128, S], BF16, "kth") for _ in range(2)]
            QTh = [ph.sb([128, S], BF16, "qth") for _ in range(2)]
            Vh = [ph.sb([128, 32, 129], BF16, "vh") for _ in range(2)]
            r_kq = [Res() for _ in range(2)]
            r_vh = [Res() for _ in range(2)]
            ds_h = [sc.new_ds() for _ in range(2)]
            for b in range(2):
                sc.memset("pool", Vh[b][:, :, 128:129], 1.0, W=[r_vh[b]])
            NPT = 6
            Tmul2 = ph.sb([128, H, 2, 256], F32, "tmul2")
            for h in range(H):
                for br in range(2):
                    sc.copy("pool", Tmul2[:, h, br, :], Tmul[:, h, :], R=[r_t], W=[r_t])
            Pt = [ph.sb([128, 2, 512], BF16, "pt") for _ in range(NPT)]
            r_pt = [Res() for _ in range(NPT)]
            Osb2 = [ph.sb([128, 9, 129], F32, "osb") for _ in range(2)]
            r_osb2 = [Res() for _ in range(2)]
            rl = ph.sb([128, 16], F32, "rl")
            r_rl = Res()
            ot4 = ph.sb([128, 4, 128], F32, "ot4")
            r_ot4 = Res()
            t4 = ph.sb([128, 4, 128], F32, "t4")
            r_t4 = Res()
            sq4 = ph.sb([128, 4, 128], F32, "sq4")
            r_sq4 = Res()
            ss = ph.sb([128, 16], F32, "ss")
            r_ss = Res()
            yb4 = ph.sb([128, 4, 128], BF16, "yb4")
            r_yb4 = Res()
            ybo = [ph.sb([128, TT], BF16, "ybo") for _ in range(2)]
            r_ybo = [Res() for _ in range(2)]
            ds_ybo = [sc.new_ds("pool") for _ in range(2)]
            Sb = [ph.ps([128, 2, 512], F32, "sb") for _ in range(2)]
            r_sb = [Res() for _ in range(2)]
            Ob = [ph.ps([128, 512], F32, "ob") for _ in range(3)]
            r_ob = [Res() for _ in range(3)]
            Tb = ph.ps([128, 512], BF16, "tb")
            r_tb = Res()

            def oreg(br, qb):
                i = br * 4 + qb
                return i // 3, (i % 3) * 129

            Vv = Vt.rearrange("(n p) c -> p n c", p=128)

            def load_head(h):
                b = h % 2
                sc.dma("sp", KTh[b][:], KT[h * 128:(h + 1) * 128, :], W=[r_kq[b]], ds=ds_h[b])
                sc.dma("sp", QTh[b][:], QT[h * 128:(h + 1) * 128, :], W=[r_kq[b]], ds=ds_h[b])
                sc.dma("sp", Vh[b][:, :, 0:128], Vv[:, :, h * 128:(h + 1) * 128], W=[r_vh[b]], ds=ds_h[b])

            load_head(0)
            pti = 0
            ci = 0
            pend = []
            items = [(h, c, j) for h in range(H) for c in range(NT) for j in range(4 * c + 4)]

            def emit_S(it):
                h, c, j = it
                hb_ = h % 2
                qlo = max(4 * c, j)
                off = (qlo - 4 * c) * 128
                for br in range(2):
                    sc.mm(Sb[j % 2][:, br, off:512],
                          KTh[hb_][br * 64:(br + 1) * 64, j * 128:(j + 1) * 128],
                          QTh[hb_][br * 64:(br + 1) * 64, c * 512 + off:(c + 1) * 512],
                          True, True, R=[r_kq[hb_]], W=[r_sb[j % 2]])

            emit_S(items[0])
            emit_S(items[1])
            touched = set()
            for ii, (h, c, j) in enumerate(items):
                hb_ = h % 2
                nj = 4 * c + 4
                if c == 0 and j == 0 and h + 1 < H:
                    load_head(h + 1)
                if j == 0:
                    touched = set()
                qlo = max(4 * c, j)
                off = (qlo - 4 * c) * 128
                p = pti % NPT
                pti += 1
                sc.act(Pt[p][:, :, off:512], Sb[j % 2][:, :, off:512], AF.Exp,
                       R=[r_sb[j % 2], r_c], W=[r_pt[p]], bias=b31[:, h:h + 1])
                q0 = max(j, 4 * c)
                q1 = min(j + 1, 4 * c + 3)
                if q0 <= q1:
                    a0 = (q0 - 4 * c) * 128
                    a1 = (q1 - 4 * c + 1) * 128
                    t0 = (q0 - j) * 128
                    t1 = (q1 - j + 1) * 128
                    sc.tt("dve", Pt[p][:, :, a0:a1], Pt[p][:, :, a0:a1], Tmul2[:, h, :, t0:t1], ALU.mult,
                          R=[r_t], W=[r_pt[p]])
                if ii + 2 < len(items):
                    emit_S(items[ii + 2])
                for br in range(2):
                    for qb in range(qlo, 4 * c + 4):
                        bk, co = oreg(br, qb - 4 * c)
                        first = bk not in touched
                        touched.add(bk)
                        sc.mm(Ob[bk][:, co:co + 129], Pt[p][:, br, (qb - 4 * c) * 128:(qb - 4 * c + 1) * 128],
                              Vh[hb_][:, j, :], first, False, R=[r_pt[p], r_vh[hb_]], W=[r_ob[bk]],
                              sig=(qb == 4 * c + 3), skip=True)
                if pend and j == (0, min(5, nj - 2), min(7, nj - 1))[3 - len(pend)]:
                    pend.pop(0)()
                if j < nj - 1:
                    continue
                ob_ = ci % 2
                ci += 1
                Osb = Osb2[ob_]
                r_osb = r_osb2[ob_]
                for bk in range(3):
                    n = 387 if bk < 2 else 258
                    sc.copy("dve", Osb[:, 3 * bk:3 * bk + n // 129, :], Ob[bk][:, 0:n].rearrange("p (a b) -> p a b", b=129),
                            R=[r_ob[bk]], W=[r_osb])

                def stage1(Osb=Osb, r_osb=r_osb):
                    sc.recip(rl[:, 0:8], Osb[:, 0:8, 128], R=[r_osb], W=[r_rl])
                    sc.tt("dve", ot4[:], Osb[:, 0:4, 0:128], rl[:, 0:4].unsqueeze(2).to_broadcast([128, 4, 128]), ALU.mult,
                          R=[r_osb, r_rl], W=[r_ot4])
                    sc.ts("dve", rl[:, 8:12], rl[:, 4:8], lsum[:, 4:5], None, ALU.mult, R=[r_rl, r_lam], W=[r_rl])
                    sc.tt("dve", t4[:], Osb[:, 4:8, 0:128], rl[:, 8:12].unsqueeze(2).to_broadcast([128, 4, 128]), ALU.mult,
                          R=[r_osb, r_rl], W=[r_t4])
                    sc.tt("dve", ot4[:], ot4[:], t4[:], ALU.add, R=[r_t4], W=[r_ot4])
                    sc.tt("dve", sq4[:], ot4[:], ot4[:], ALU.mult, R=[r_ot4], W=[r_sq4])
                    sc.op("dve", lambda hh, o=ss[:, 0:4], a=sq4[:]: hh.reduce_sum(out=o, in_=a, axis=mybir.AxisListType.X),
                          R=[r_sq4], W=[r_ss])

                def stage2():
                    sc.ts("dve", ss[:, 4:8], ss[:, 0:4], 1.0 / 128, EPS, ALU.mult, ALU.add, R=[r_ss], W=[r_ss])
                    sc.act(ss[:, 8:12], ss[:, 4:8], AF.Ln, R=[r_ss], W=[r_ss])
                    sc.act(ss[:, 12:16], ss[:, 8:12], AF.Exp, R=[r_ss], W=[r_ss], scale=-0.5)
                    sc.tt("pool", t4[:], ot4[:], ss[:, 12:16].unsqueeze(2).to_broadcast([128, 4, 128]), ALU.mult,
                          R=[r_ot4, r_ss], W=[r_t4])
                    sc.tt("pool", yb4[:], t4[:], sgs[:].unsqueeze(1).to_broadcast([128, 4, 128]), ALU.mult,
                          R=[r_t4, r_sg], W=[r_yb4])

                def stage3(h=h, c=c, ob_=ob_):
                    for qb in range(4):
                        sc.tr(Tb[:, qb * 128:(qb + 1) * 128], yb4[:, qb, :], identb[:], R=[r_yb4, r_idb], W=[r_tb], sig=(qb == 3))
                    sc.copy("dve", ybo[ob_][:], Tb[:], R=[r_tb], W=[r_ybo[ob_]])
                    sc.dma("pool", ybT[h * 128:(h + 1) * 128, c * TT:(c + 1) * TT], ybo[ob_][:], R=[r_ybo[ob_]], ds=ds_ybo[ob_])

                pend[:] = [stage1, stage2, stage3]
            for st_ in pend:
                st_()
            sc.flush()
            ph.close()

        if want("D"):
            ph = Phase(nc, "d%d" % l)
            Wb = ph.sb([128, 8, 1024], BF16, "wb")
            Wgb = ph.sb([128, 8, 1024], BF16, "wgb")
            Wo = ph.sb([128, 8, 1024], BF16, "wo")
            bnd = list(range(0, 1025, 256))
            wgb = WT(Wgb, w_gate[l][:, 1024:2048], bnd)
            wb = WT(Wb, w_b[l], bnd)
            wo = WT(Wo, w_out[l], bnd)
            for _ in range(4):
                wgb.issue(1)
                wb.issue(1)
            wo.issue()
            hb = [ph.sb([128, 8, TT], BF16, "hb") for _ in range(2)]
            yt = [ph.sb([128, 8, TT], BF16, "yt") for _ in range(2)]
            mt = [ph.sb([128, 8, TT], F32, "mt") for _ in range(2)]
            NX = 3
            xt = [ph.sb([128, 8, TT], F32, "xt") for _ in range(NX)]
            r_in = [Res() for _ in range(2)]
            r_yt = [Res() for _ in range(2)]
            r_mt = [Res() for _ in range(2)]
            r_xt = [Res() for _ in range(NX)]
            ds_in = [sc.new_ds() for _ in range(2)]
            ds_yt = [sc.new_ds() for _ in range(2)]
            ds_mt = [sc.new_ds() for _ in range(2)]
            ds_x = [sc.new_ds() for _ in range(NX)]
            ds_xo = [sc.new_ds("pool") for _ in range(NX)]
            gbt = [ph.sb([128, TT], F32, "gbt") for _ in range(2)]
            r_gb = [Res() for _ in range(2)]
            tmp = [ph.sb([128, TT], F32, "tmp") for _ in range(2)]
            r_tmp = [Res() for _ in range(2)]
            mrg = [ph.sb([128, 8, TT], BF16, "mrg") for _ in range(2)]
            r_mrg = [[Res() for _ in range(8)] for _ in range(2)]
            banks = [ph.ps([128, 512], F32, "bk") for _ in range(8)]
            r_bk = [Res() for _ in range(8)]
            bi = 0
            hTv, yTv, mTv, xTv = chunked(hT), chunked(ybT), chunked(m1T), chunked(xT)

            def loads(t):
                b = t % 2
                tok = slice(t * TT, (t + 1) * TT)
                sc.dma("sp", hb[b][:], hTv[:, :, tok], W=[r_in[b]], ds=ds_in[b])
                sc.dma("sp", yt[b][:], yTv[:, :, tok], W=[r_yt[b]], ds=ds_yt[b])
                sc.dma("sp", mt[b][:], mTv[:, :, tok], W=[r_mt[b]], ds=ds_mt[b])
                sc.dma("sp", xt[t % NX][:], xTv[:, :, tok], W=[r_xt[t % NX]], ds=ds_x[t % NX])

            def out_proj(t):
                nonlocal bi
                m = t % 2
                x = xt[t % NX]
                for e in range(8):
                    k = bi % 8
                    bi += 1
                    for kc in range(8):
                        sc.mm(banks[k][:], Wo[:, kc, e * 128:(e + 1) * 128], mrg[m][:, kc, :], kc == 0, kc == 7,
                              R=[wo.res(e * 128), r_mrg[m][kc]], W=[r_bk[k]])
                    sc.tt("dve", x[:, e, :], banks[k][:], x[:, e, :], ALU.add, R=[r_bk[k]], W=[r_xt[t % NX]])
                sc.dma("pool", xTv[:, :, t * TT:(t + 1) * TT], x[:], R=[r_xt[t % NX]], ds=ds_xo[t % NX])

            loads(0)
            ei = 0
            for t in range(NT):
                b = t % 2
                if t + 1 < NT:
                    loads(t + 1)
                for e in range(8):
                    eb = ei % 2
                    ei += 1
                    kB = bi % 8
                    bi += 1
                    for kc in range(8):
                        sc.mm(banks[kB][:], Wgb[:, kc, e * 128:(e + 1) * 128], hb[b][:, kc, :], kc == 0, kc == 7,
                              R=[wgb.res(e * 128), r_in[b]], W=[r_bk[kB]])
                    sc.act(gbt[eb][:], banks[kB][:], AF.Sigmoid, R=[r_bk[kB]], W=[r_gb[eb]])
                    kA = bi % 8
                    bi += 1
                    for kc in range(8):
                        sc.mm(banks[kA][:], Wb[:, kc, e * 128:(e + 1) * 128], yt[b][:, kc, :], kc == 0, kc == 7,
                              R=[wb.res(e * 128), r_yt[b]], W=[r_bk[kA]])
                    sc.tt("dve", tmp[eb][:], banks[kA][:], gbt[eb][:], ALU.mult, R=[r_bk[kA], r_gb[eb]], W=[r_tmp[eb]])
                    sc.tt("dve", mrg[b][:, e, :], tmp[eb][:], mt[b][:, e, :], ALU.add, R=[r_tmp[eb], r_mt[b]], W=[r_mrg[b][e]])
                if t >= 1:
                    out_proj(t - 1)
            out_proj(NT - 1)
            sc.flush()
            ph.close()

        if want("E"):
            ph = Phase(nc, "e%d" % l)
            NTE = S // TE
            Wup = ph.sb([128, 8, 2 * DFF], BF16, "wup")
            Wdn = ph.sb([128, NF, D], BF16, "wdn")
            wupa = WT(Wup, w_up[l], list(range(0, DFF + 1, 256)))
            wupb = WT(Wup, w_up[l], list(range(DFF, 2 * DFF + 1, 256)))
            wdn = WT(Wdn, w_down[l], list(range(0, 1025, 256)))
            cv = ph.sb([128, NF, 4], F32, "cv")
            g2 = ph.sb([128, 8], F32, "g2")
            ones_bf = ph.sb([128, 128], BF16, "ones")
            halo = ph.sb([128, NF, 2], F32, "halo")
            r_c, r_ones = Res(), Res()
            r_halo = [Res() for _ in range(NF)]
            ds_c = sc.new_ds()
            sc.memset("pool", ones_bf[:], 1.0, W=[r_ones])
            sc.memset("pool", halo[:], 0.0, W=r_halo)
            for _ in range(NF // 2):
                wupa.issue(1)
                wupb.issue(1)
            wdn.issue()
            NX = 3
            xt = [ph.sb([128, 8, TE], F32, "xt") for _ in range(NX)]
            r_xt = [Res() for _ in range(NX)]
            ds_xt = [sc.new_ds() for _ in range(NX)]
            ds_xo = [sc.new_ds("pool") for _ in range(NX)]
            sq = ph.sb([128, 8, TE], BF16, "sq")
            r_sq = Res()
            rstd = ph.sb([128, TE], F32, "rstd")
            r_rstd = Res()
            h2 = [ph.sb([128, 8, TE], BF16, "h2") for _ in range(2)]
            r_h2 = [Res() for _ in range(2)]
            ptmp = [ph.sb([128, TE], F32, "ptmp") for _ in range(2)]
            r_ptmp = [Res() for _ in range(2)]
            NA = 3
            asb = [ph.sb([128, TE + 2], F32, "asb") for _ in range(NA)]
            r_asb = [Res() for _ in range(NA)]
            t1 = [ph.sb([128, TE], F32, "t1") for _ in range(NA)]
            r_t1 = [Res() for _ in range(NA)]
            gl = [ph.sb([128, TE], F32, "gl") for _ in range(NA)]
            r_gl = [Res() for _ in range(NA)]
            gT = [ph.sb([128, NF, TE], BF16, "gT") for _ in range(2)]
            r_gT = [[Res() for _ in range(NF)] for _ in range(2)]
            banks = [ph.ps([128, 512], F32, "bk") for _ in range(8)]
            r_bk = [Res() for _ in range(8)]
            NUPB = 6
            st = {"bi": 0, "ai": 0, "di": 0}
            xTv = chunked(xT)

            def upbank():
                k = st["bi"] % NUPB
                st["bi"] += 1
                return k

            def pro_sq(t):
                x = xt[t % NX]
                for kc in range(8):
                    sc.act(sq[:, kc, :], x[:, kc, :], AF.Square, R=[r_xt[t % NX]], W=[r_sq])

            def pro_h(t):
                b = t % 2
                x = xt[t % NX]
                k = upbank()
                for kc in range(8):
                    sc.mm(banks[k][:, 0:TE], ones_bf[:], sq[:, kc, :], kc == 0, kc == 7, R=[r_sq, r_ones], W=[r_bk[k]])
                sc.ts("dve", rstd[:], banks[k][:, 0:TE], 1.0 / D, EPS, ALU.mult, ALU.add, R=[r_bk[k]], W=[r_rstd])
                sc.act(rstd[:], rstd[:], AF.Sqrt, R=[r_rstd], W=[r_rstd])
                sc.recip(rstd[:], rstd[:], R=[r_rstd], W=[r_rstd])
                if t == 0:
                    for kc in range(8):
                        pro_h2(t, kc)

            def pro_h2(t, kc):
                b = t % 2
                x = xt[t % NX]
                sc.stt("dve", h2[b][:, kc, :], x[:, kc, :], g2[:, kc:kc + 1], rstd[:], ALU.mult, ALU.mult,
                       R=[r_xt[t % NX], r_rstd, r_c], W=[r_h2[b]])

            def down_steps(t):
                b = t % 2
                x = xt[t % NX]
                rx = r_xt[t % NX]
                n = 0
                for e in range(8):
                    k = 6 + st["di"] % 2
                    st["di"] += 1
                    for f in range(NF):
                        sc.mm(banks[k][:, 0:TE], Wdn[:, f, e * 128:(e + 1) * 128], gT[b][:, f, :], f == 0, f == NF - 1,
                              R=[wdn.res(e * 128), r_gT[b][f]], W=[r_bk[k]])
                        n += 1
                        if f == NF - 1:
                            sc.tt("dve", x[:, e, :], banks[k][:, 0:TE], x[:, e, :], ALU.add, R=[r_bk[k]], W=[rx])
                            if e == 7:
                                sc.dma("pool", xTv[:, :, t * TE:(t + 1) * TE], x[:], R=[rx], ds=ds_xo[t % NX])
                        if n % 8 == 0:
                            yield

            def fin_f(t, f, a):
                b = t % 2
                kB = fin_f.kb[(t, f)]
                sc.act(gl[a][:], t1[a][:], AF.Gelu_apprx_tanh, R=[r_t1[a]], W=[r_gl[a]])
                sc.tt("dve", gT[b][:, f, :], banks[kB][:, 0:TE], gl[a][:], ALU.mult, R=[r_bk[kB], r_gl[a]], W=[r_gT[b][f]])

            fin_f.kb = {}

            sc.dma("sp", xt[0][:], xTv[:, :, 0:TE], W=[r_xt[0]], ds=ds_xt[0])
            sc.dma("sp", cv[:], convT[l], W=[r_c], ds=ds_c)
            sc.dma("sp", g2[:], g2T[l], W=[r_c], ds=ds_c)
            pro_sq(0)
            pro_h(0)
            dgen = None
            prev = None
            for t in range(NTE):
                b = t % 2
                if t + 1 < NTE:
                    n1 = (t + 1) % NX
                    sc.dma("sp", xt[n1][:], xTv[:, :, (t + 1) * TE:(t + 2) * TE], W=[r_xt[n1]], ds=ds_xt[n1])
                for f in range(NF):
                    if t + 1 < NTE and 5 <= f < 13:
                        sc.act(sq[:, f - 5, :], xt[(t + 1) % NX][:, f - 5, :], AF.Square, R=[r_xt[(t + 1) % NX]], W=[r_sq])
                    if t + 1 < NTE and f == 13:
                        pro_h(t + 1)
                    if t + 1 < NTE and 14 <= f < 22:
                        pro_h2(t + 1, f - 14)
                    a = st["ai"] % NA
                    st["ai"] += 1
                    kA = upbank()
                    for kc in range(8):
                        sc.mm(banks[kA][:, 0:TE], Wup[:, kc, f * 128:(f + 1) * 128], h2[b][:, kc, :], kc == 0, kc == 7,
                              R=[wupa.res(f * 128), r_h2[b]], W=[r_bk[kA]])
                    kB = upbank()
                    fin_f.kb[(t, f)] = kB
                    for kc in range(8):
                        sc.mm(banks[kB][:, 0:TE], Wup[:, kc, DFF + f * 128:DFF + (f + 1) * 128], h2[b][:, kc, :], kc == 0, kc == 7,
                              R=[wupb.res(DFF + f * 128), r_h2[b]], W=[r_bk[kB]])
                    if dgen is not None:
                        next(dgen, None)
                    sc.act(t1[a][:], banks[kA][:, 0:TE], AF.Identity, R=[r_bk[kA], r_c], W=[r_t1[a]],
                           scale=cv[:, f, 2:3], bias=cv[:, f, 3:4])
                    sc.copy("act", asb[a][:, 2:TE + 2], banks[kA][:, 0:TE], R=[r_bk[kA]], W=[r_asb[a]])
                    sc.copy("act", asb[a][:, 0:2], halo[:, f, :], R=[r_halo[f]], W=[r_asb[a]])
                    sc.copy("act", halo[:, f, :], asb[a][:, TE:TE + 2], R=[r_asb[a]], W=[r_halo[f]])
                    sc.stt("dve", t1[a][:], asb[a][:, 1:TE + 1], cv[:, f, 1:2], t1[a][:], ALU.mult, ALU.add,
                           R=[r_asb[a], r_c], W=[r_t1[a]])
                    sc.stt("dve", t1[a][:], asb[a][:, 0:TE], cv[:, f, 0:1], t1[a][:], ALU.mult, ALU.add,
                           R=[r_asb[a], r_c], W=[r_t1[a]])
                    if prev is not None:
                        fin_f(*prev)
                    prev = (t, f, a)
                fin_f(*prev)
                prev = None
                if dgen is not None:
                    for _ in dgen:
                        pass
                dgen = down_steps(t)
            for _ in dgen:
                pass
            sc.flush()
            ph.close()


    if want("F"):
        ph = Phase(nc, "f")
        ident = ph.sb([128, 128], F32, "ident")
        fg = ph.sb([128, D], F32, "fg")
        r_c = Res()
        ds_c = sc.new_ds()
        sc.dma("sp", ident[:], c_ident, W=[r_c], ds=ds_c)
        sc.dma("sp", fg[:], fing.partition_broadcast(128), W=[r_c], ds=ds_c)
        xi = [ph.sb([128, 8, 128], F32, "xi") for _ in range(3)]
        r_xi = [Res() for _ in range(3)]
        ds_xi = [sc.new_ds() for _ in range(3)]
        junk = ph.sb([128, 512], F32, "junk")
        r_junk = Res()
        ss = [ph.sb([128, 8], F32, "ss") for _ in range(2)]
        r_ss = [Res() for _ in range(2)]
        osb = [ph.sb([128, D], F32, "osb") for _ in range(2)]
        r_osb = [Res() for _ in range(2)]
        ds_o = [sc.new_ds("pool") for _ in range(2)]
        banks = [ph.ps([128, 512], F32, "bk") for _ in range(8)]
        r_bk = [Res() for _ in range(8)]
        xTv = chunked(xT)
        NB = S // 128
        sc.dma("sp", xi[0][:], xTv[:, :, 0:128], W=[r_xi[0]], ds=ds_xi[0])
        sc.dma("sp", xi[1][:], xTv[:, :, 128:256], W=[r_xi[1]], ds=ds_xi[1])
        for tb in range(NB):
            b3 = tb % 3
            b = tb % 2
            if tb + 2 < NB:
                n3 = (tb + 2) % 3
                sc.dma("sp", xi[n3][:], xTv[:, :, (tb + 2) * 128:(tb + 3) * 128], W=[r_xi[n3]], ds=ds_xi[n3])
            k0 = (tb * 2) % 8
            for hf in range(2):
                k = k0 + hf
                for c4 in range(4):
                    kc = hf * 4 + c4
                    sc.tr(banks[k][:, c4 * 128:(c4 + 1) * 128], xi[b3][:, kc, :], ident[:], R=[r_xi[b3], r_c], W=[r_bk[k]], sig=(c4 == 3))
                sc.act(junk[:], banks[k][:], AF.Square, R=[r_bk[k]], W=[r_junk, r_ss[b]], accum_out=ss[b][:, hf:hf + 1])
            sc.tt("dve", ss[b][:, 2:3], ss[b][:, 0:1], ss[b][:, 1:2], ALU.add, R=[r_ss[b]], W=[r_ss[b]])
            sc.ts("dve", ss[b][:, 3:4], ss[b][:, 2:3], 1.0 / D, EPS, ALU.mult, ALU.add, R=[r_ss[b]], W=[r_ss[b]])
            sc.act(ss[b][:, 4:5], ss[b][:, 3:4], AF.Sqrt, R=[r_ss[b]], W=[r_ss[b]])
            sc.recip(ss[b][:, 5:6], ss[b][:, 4:5], R=[r_ss[b]], W=[r_ss[b]])
            for hf in range(2):
                k = k0 + hf
                sc.stt("dve", osb[b][:, hf * 512:(hf + 1) * 512], banks[k][:], ss[b][:, 5:6], fg[:, hf * 512:(hf + 1) * 512],
                       ALU.mult, ALU.mult, R=[r_bk[k], r_ss[b], r_c], W=[r_osb[b]])
            sc.dma("pool", out[tb * 128:(tb + 1) * 128, :], osb[b][:], R=[r_osb[b]], ds=ds_o[b])
        sc.flush()
        ph.close()
    return nc


def _t5_bucket(rel):
    n = np.maximum(rel, 0)
    max_exact = 16
    nf = np.maximum(n, 1).astype(np.float32)
    large = max_exact + (np.log(nf / np.float32(max_exact)) / np.float32(math.log(128 / max_exact))
                         * np.float32(32 - max_exact)).astype(np.int32)
    large = np.minimum(large, 31)
    return np.where(n < max_exact, n, large)


def host_layout(inp):
    f = lambda a: np.ascontiguousarray(np.asarray(a, dtype=np.float32))
    shared = {}
    for k in ("w_in", "w_gate", "w_a", "w_b", "w_out", "w_up", "w_down", "rel_bias"):
        shared[k] = f(inp[k])
    shared["g1T"] = f(np.asarray(inp["norm1_g"]).reshape(L, 8, 128).transpose(0, 2, 1))
    shared["g2T"] = f(np.asarray(inp["norm2_g"]).reshape(L, 8, 128).transpose(0, 2, 1))
    shared["vng"] = f(inp["gmlp_vnorm_g"])
    shared["wsT"] = f(np.asarray(inp["gmlp_ws"]).transpose(0, 1, 3, 2))
    shared["gmlp_b"] = f(np.asarray(inp["gmlp_b"]).reshape(L, 1, D))
    shared["lamv"] = f(np.stack([np.asarray(inp["lam_q1"]), np.asarray(inp["lam_k1"]),
                                 np.asarray(inp["lam_q2"]), np.asarray(inp["lam_k2"])], axis=1))
    shared["subln_g"] = f(inp["subln_g"])
    cw = np.asarray(inp["conv_w"]).reshape(L, 3, NF, 128)
    cb = np.asarray(inp["conv_b"]).reshape(L, 1, NF, 128)
    shared["convT"] = f(np.concatenate([cw, cb], axis=1).transpose(0, 3, 2, 1))
    shared["final_g"] = f(np.asarray(inp["final_g"]).reshape(1, D))
    kk = np.arange(128)[:, None]
    cc = np.arange(256)[None, :]
    rel = cc - kk
    bucket = _t5_bucket(rel)
    shared["btab"] = f(np.asarray(inp["rel_bias"])[bucket].transpose(0, 2, 1))
    shared["c_mask"] = f((rel >= 0).astype(np.float32))
    shared["c_ident"] = f(np.eye(128))
    shared["c_tri"] = f((np.arange(128)[:, None] <= np.arange(128)[None, :]).astype(np.float32))
    return shared


_NC_CACHE = {}


def kernel(**inputs):
    shared = host_layout(inputs)
    x = np.asarray(inputs["x"], dtype=np.float32)
    if "nc" not in _NC_CACHE:
        _NC_CACHE["nc"] = build_program()
    nc = _NC_CACHE["nc"]
    in_maps = []
    for c in range(NCORES):
        m = dict(shared)
        m["x"] = np.ascontiguousarray(x[c])
        in_maps.append(m)
    res = run_bass_kernel_spmd(nc, in_maps, core_ids=list(range(NCORES)))
    return np.stack([np.asarray(r["out"]) for r in res.results], axis=0).astype(np.float32)
```
